# Optimizing a Trainium2 kernel written in Bass

```python
import math
import jax, jax.numpy as jnp
from jax import lax
import numpy as np

D_MODEL = 1024
BATCH = 8
SEQ = 2048
DEPTH = 4

CHUNK = 64
N_META = 16
N_A_LAYERS = DEPTH // 2
N_B_LAYERS = DEPTH - N_A_LAYERS
D_RNN = 3 * D_MODEL // 2
N_LRU_BLOCKS = 16
LRU_BLOCK = D_RNN // N_LRU_BLOCKS
LRU_C = 8.0
CONV_A_WIDTH = 4
N_FOX_HEADS = 16
FOX_HEAD_DIM = D_MODEL // N_FOX_HEADS
Q_BLOCK = 128
D_FF = ((8 * D_MODEL // 3 + 255) // 256) * 256
CONV_F_WIDTH = 3
DN_ALPHA = (2 * DEPTH) ** 0.25
DN_BETA = (8 * DEPTH) ** -0.25
LN_EPS = 1e-5

kernel_name = "yoco_rglru_fox_convffn_deepnorm"


def layer_norm(x, g, b):
    xf = x.astype(jnp.float32)
    mu = jnp.mean(xf, axis=-1, keepdims=True)
    var = jnp.mean(jnp.square(xf - mu), axis=-1, keepdims=True)
    y = (xf - mu) * lax.rsqrt(var + LN_EPS)
    return (y * g.astype(jnp.float32) + b.astype(jnp.float32)).astype(x.dtype)


def causal_dwconv(x, w, b):
    width = w.shape[0]
    length = x.shape[1]
    xp = jnp.pad(x, ((0, 0), (width - 1, 0), (0, 0)))
    y = b
    for k in range(width):
        y = y + xp[:, k:k + length] * w[k]
    return y


def rg_lru(x, w_r, b_r, w_i, b_i, lam):
    bsz, length, width = x.shape
    xb = x.reshape(bsz, length, N_LRU_BLOCKS, LRU_BLOCK)
    r = jax.nn.sigmoid(jnp.einsum('blnc,ncd->blnd', xb, w_r).reshape(bsz, length, width) + b_r)
    i = jax.nn.sigmoid(jnp.einsum('blnc,ncd->blnd', xb, w_i).reshape(bsz, length, width) + b_i)
    log_a = -LRU_C * r.astype(jnp.float32) * jax.nn.softplus(-lam.astype(jnp.float32))
    a = jnp.exp(log_a)
    u = jnp.sqrt(-jnp.expm1(2.0 * log_a)) * (i * x).astype(jnp.float32)

    def combine(left, right):
        a1, b1 = left
        a2, b2 = right
        return a1 * a2, a2 * b1 + b2

    _, h = lax.associative_scan(combine, (a, u), axis=1)
    return h.astype(x.dtype)


def recurrent_mixer(x, w_in, conv_w, conv_b, w_r, b_r, w_i, b_i, lam, w_out):
    gr = x @ w_in
    gate, rec = gr[..., :D_RNN], gr[..., D_RNN:]
    rec = causal_dwconv(rec, conv_w, conv_b)
    h = rg_lru(rec, w_r, b_r, w_i, b_i, lam)
    return (jax.nn.gelu(gate) * h) @ w_out


def conv_ffn(x, w_in, conv_w, conv_b, w_out):
    h = causal_dwconv(x @ w_in, conv_w, conv_b)
    gate, val = h[..., :D_FF], h[..., D_FF:]
    return (jax.nn.gelu(gate) * val) @ w_out


def to_heads_padded(t, lp):
    bsz, length, _ = t.shape
    t = t.reshape(bsz, length, N_FOX_HEADS, FOX_HEAD_DIM).transpose(0, 2, 1, 3)
    return jnp.pad(t, ((0, 0), (0, 0), (0, lp - length), (0, 0)))


def shared_kv(x, kv_w, f_b):
    length = x.shape[1]
    lp = -(-length // Q_BLOCK) * Q_BLOCK
    z = x @ kv_w
    k = to_heads_padded(z[..., :D_MODEL], lp)
    v = to_heads_padded(z[..., D_MODEL:2 * D_MODEL], lp)
    log_f = jax.nn.log_sigmoid(z[..., 2 * D_MODEL:].astype(jnp.float32) + f_b.astype(jnp.float32))
    c = jnp.cumsum(log_f, axis=1).transpose(0, 2, 1)
    c = jnp.pad(c, ((0, 0), (0, 0), (0, lp - length)), mode='edge')
    return k, v, c


def forgetting_attention(q, k, v, c):
    lp = q.shape[2]
    scale = q.shape[-1] ** -0.5
    outs = []
    for q0 in range(0, lp, Q_BLOCK):
        end = q0 + Q_BLOCK
        s = jnp.einsum('bhqd,bhkd->bhqk', q[:, :, q0:end], k[:, :, :end]).astype(jnp.float32) * scale
        s = s + c[:, :, q0:end, None] - c[:, :, None, :end]
        mask = jnp.arange(end)[None, :] <= jnp.arange(q0, end)[:, None]
        s = jnp.where(mask, s, -jnp.inf)
        p = jax.nn.softmax(s, axis=-1).astype(v.dtype)
        outs.append(jnp.einsum('bhqk,bhkd->bhqd', p, v[:, :, :end]))
    return jnp.concatenate(outs, axis=2)


def fox_mixer(x, w_in, w_out, k, v, c):
    bsz, length, _ = x.shape
    qg = x @ w_in
    q = to_heads_padded(qg[..., :D_MODEL], k.shape[2])
    o = forgetting_attention(q, k, v, c)[:, :, :length]
    o = o.transpose(0, 2, 1, 3).reshape(bsz, length, D_MODEL)
    return (o * jax.nn.sigmoid(qg[..., D_MODEL:])) @ w_out


def setup_inputs(seed: int = 0) -> dict:
    key = jax.random.key(seed)
    ks = jax.random.split(key, 24)
    f32 = jnp.float32
    d = D_MODEL

    def nrm(k, shape, scale):
        return jax.random.normal(k, shape, f32) * scale

    u = jax.random.uniform(ks[9], (N_A_LAYERS, D_RNN), f32, 0.9, 0.999)
    a0 = u ** (1.0 / LRU_C)
    lam = jnp.log(a0) - jnp.log1p(-a0)

    kv_w = jnp.concatenate([
        nrm(ks[11], (d, d), d ** -0.5),
        nrm(ks[12], (d, d), d ** -0.5 * DN_BETA),
        nrm(ks[13], (d, N_FOX_HEADS), d ** -0.5),
    ], axis=1)

    return {
        "x": nrm(ks[0], (BATCH, SEQ, d), 1.0),
        "meta": nrm(ks[1], (N_META, d), 1.0),
        "a_w_in": nrm(ks[2], (N_A_LAYERS, d, 2 * D_RNN), d ** -0.5),
        "a_conv_w": nrm(ks[3], (N_A_LAYERS, CONV_A_WIDTH, D_RNN), CONV_A_WIDTH ** -0.5),
        "a_conv_b": nrm(ks[4], (N_A_LAYERS, D_RNN), 0.02),
        "a_w_r": nrm(ks[5], (N_A_LAYERS, N_LRU_BLOCKS, LRU_BLOCK, LRU_BLOCK), LRU_BLOCK ** -0.5),
        "a_b_r": nrm(ks[6], (N_A_LAYERS, D_RNN), 0.02),
        "a_w_i": nrm(ks[7], (N_A_LAYERS, N_LRU_BLOCKS, LRU_BLOCK, LRU_BLOCK), LRU_BLOCK ** -0.5),
        "a_b_i": nrm(ks[8], (N_A_LAYERS, D_RNN), 0.02),
        "a_lambda": lam,
        "a_w_out": nrm(ks[10], (N_A_LAYERS, D_RNN, d), D_RNN ** -0.5 * DN_BETA),
        "kv_w": kv_w,
        "kv_f_b": jax.random.uniform(ks[14], (N_FOX_HEADS,), f32, 1.0, 4.0),
        "b_w_in": nrm(ks[15], (N_B_LAYERS, d, 2 * d), d ** -0.5),
        "b_w_out": nrm(ks[16], (N_B_LAYERS, d, d), d ** -0.5 * DN_BETA),
        "f_w_in": nrm(ks[17], (DEPTH, d, 2 * D_FF), d ** -0.5),
        "f_conv_w": nrm(ks[18], (DEPTH, CONV_F_WIDTH, 2 * D_FF), CONV_F_WIDTH ** -0.5),
        "f_conv_b": nrm(ks[19], (DEPTH, 2 * D_FF), 0.02),
        "f_w_out": nrm(ks[20], (DEPTH, D_FF, d), D_FF ** -0.5 * DN_BETA),
        "ln1_g": 1.0 + nrm(ks[21], (DEPTH, d), 0.02),
        "ln1_b": nrm(ks[22], (DEPTH, d), 0.02),
        "ln2_g": 1.0 + nrm(ks[23], (DEPTH, d), 0.02),
        "ln2_b": nrm(jax.random.fold_in(key, 99), (DEPTH, d), 0.02),
    }


def reference(x, meta, a_w_in, a_conv_w, a_conv_b, a_w_r, a_b_r, a_w_i, a_b_i, a_lambda, a_w_out,
              kv_w, kv_f_b, b_w_in, b_w_out, f_w_in, f_conv_w, f_conv_b, f_w_out,
              ln1_g, ln1_b, ln2_g, ln2_b):
    bsz = x.shape[0]
    h = jnp.concatenate([jnp.broadcast_to(meta.astype(x.dtype), (bsz, N_META, D_MODEL)), x], axis=1)
    k = v = c = None
    for layer in range(DEPTH):
        if layer < N_A_LAYERS:
            mix = recurrent_mixer(h, a_w_in[layer], a_conv_w[layer], a_conv_b[layer],
                                  a_w_r[layer], a_b_r[layer], a_w_i[layer], a_b_i[layer],
                                  a_lambda[layer], a_w_out[layer])
        else:
            if layer == N_A_LAYERS:
                k, v, c = shared_kv(h, kv_w, kv_f_b)
            j = layer - N_A_LAYERS
            mix = fox_mixer(h, b_w_in[j], b_w_out[j], k, v, c)
        h = layer_norm(DN_ALPHA * h + mix, ln1_g[layer], ln1_b[layer])
        ffn = conv_ffn(h, f_w_in[layer], f_conv_w[layer], f_conv_b[layer], f_w_out[layer])
        h = layer_norm(DN_ALPHA * h + ffn, ln2_g[layer], ln2_b[layer])
    return h[:, N_META:]
```

```python
import numpy as np
from contextlib import ExitStack
import concourse.bass as bass
import concourse.mybir as mybir
from concourse.bass_utils import run_bass_kernel_spmd

F32 = mybir.dt.float32
BF16 = mybir.dt.bfloat16
AF = mybir.ActivationFunctionType
ALU = mybir.AluOpType

ENGS = ["tensor", "vector", "scalar", "gpsimd", "sync"]
SEM_LIMIT = 20000

D = 1024
NK = 8
SEQ = 2048
NMETA = 16
L = SEQ + NMETA
PAD = 3
NCH = 6
N = L // NCH
DRNN = 1536
NBLK = 16
BLK = 96
DFF = 2816
NPAIR = 22
NH = 16
DH = 64
ALPHA = 8.0 ** 0.25
EPS = 1e-5
UNIT = 2064
WH = 6 * UNIT
FFN_SLICES = [[0, 1, 2, 3], [4, 5, 6, 7], [8, 9, 10, 11], [12, 13, 14, 15], [16, 17, 18], [19, 20, 21]]
NSLOT = 14
SW = 352
NTT = 17
KAUG = 70


class DSem:
    def __init__(self, name):
        self.name = name
        self.count = 0
        self.h = None


class Op:
    __slots__ = ("eng", "fn", "deps", "needs_signal", "sig", "dsem", "dma_targets")


class Prog:
    def __init__(self, nc, stack):
        self.nc = nc
        self.stack = stack
        self.ops = []
        self.last_writer = {}
        self.readers = {}
        self.dsems = []
        self.nt = 0

    def sb(self, shape, dt):
        self.nt += 1
        return self.stack.enter_context(self.nc.sbuf_tensor(f"sb{self.nt}", list(shape), dt))

    def ps(self, shape, dt=F32):
        self.nt += 1
        return self.stack.enter_context(self.nc.psum_tensor(f"ps{self.nt}", list(shape), dt))

    def dsem(self, name):
        d = DSem(name)
        self.dsems.append(d)
        return d

    def op(self, eng, fn, reads=(), writes=(), dsem=None):
        o = Op()
        o.eng = eng
        o.fn = fn
        o.needs_signal = False
        o.sig = None
        o.dsem = dsem
        deps = set()
        for r in reads:
            w = self.last_writer.get(r)
            if w is not None:
                deps.add(w)
        for w in writes:
            lw = self.last_writer.get(w)
            if lw is not None:
                deps.add(lw)
            for rd in self.readers.get(w, ()):
                deps.add(rd)
        o.dma_targets = {}
        cdeps = []
        for d in deps:
            if d.dsem is not None:
                ds = d.dsem
                o.dma_targets[ds] = max(o.dma_targets.get(ds, 0), ds.count * 16)
            else:
                if d.eng == eng and eng == "tensor":
                    continue
                d.needs_signal = True
                cdeps.append(d)
        o.deps = cdeps
        if dsem is not None:
            dsem.count += 1
        for r in reads:
            self.readers.setdefault(r, []).append(o)
        for w in writes:
            self.last_writer[w] = o
            self.readers[w] = []
        self.ops.append(o)
        return o

    def emit(self):
        nc = self.nc
        nsig = {e: 0 for e in ENGS}
        for o in self.ops:
            if o.dsem is None and o.needs_signal:
                nsig[o.eng] += 1
                o.sig = nsig[o.eng]
        esems = {}
        for e in ENGS:
            n = max(1, (nsig[e] + SEM_LIMIT - 1) // SEM_LIMIT)
            esems[e] = [self.stack.enter_context(nc.semaphore(f"s_{e}_{k}")) for k in range(n)]
        for d in self.dsems:
            d.h = self.stack.enter_context(nc.semaphore(f"d_{d.name}"))
        by_eng = {e: [] for e in ENGS}
        for o in self.ops:
            by_eng[o.eng].append(o)

        def sem_of(e, sig):
            k = (sig - 1) // SEM_LIMIT
            return esems[e][k], (sig - 1) % SEM_LIMIT + 1, (e, k)

        def run_engine(e, eng):
            waited = {}
            for o in by_eng[e]:
                w = {}
                hs = {}
                for d in o.deps:
                    h, v, key = sem_of(d.eng, d.sig)
                    if v > w.get(key, 0):
                        w[key] = v
                        hs[key] = h
                for ds, v in o.dma_targets.items():
                    key = ("d", id(ds))
                    if v > w.get(key, 0):
                        w[key] = v
                        hs[key] = ds.h
                for key, v in w.items():
                    if waited.get(key, 0) >= v:
                        continue
                    eng.wait_ge(hs[key], v)
                    waited[key] = v
                ins = o.fn(eng)
                if o.dsem is not None:
                    ins.then_inc(o.dsem.h, 16)
                elif o.needs_signal:
                    h, v, key = sem_of(o.eng, o.sig)
                    ins.then_inc(h, 1)

        with nc.Block() as block:
            for e in ENGS:
                if by_eng[e]:
                    getattr(block, e)(lambda eng, e=e: run_engine(e, eng))
        return {e: len(by_eng[e]) for e in ENGS}


def I(method, *a, **kw):
    return lambda e: getattr(e, method)(*a, **kw)


def mm(out_ap, pairs, start=True, stop=True):
    def fn(e):
        n = len(pairs)
        ins = None
        for i, (l, r) in enumerate(pairs):
            ins = e.matmul(out_ap, lhsT=l, rhs=r, start=(start and i == 0), stop=(stop and i == n - 1),
                           skip_group_check=True)
        return ins
    return fn


def build(nl=4, dbg=None):
    nc = bass.Bass("TRN2", target_bir_lowering=False)

    def din(name, shape, dt=F32):
        return nc.dram_tensor(name, list(shape), dt, kind="ExternalInput").ap()

    xT = din("xT", [D, SEQ])
    metaT = din("metaT", [D, NMETA])
    a_w_in = din("a_w_in", [2, D, 2 * DRNN])
    a_w_r = din("a_w_r", [2, NBLK, BLK, BLK])
    a_w_i = din("a_w_i", [2, NBLK, BLK, BLK])
    a_w_out = din("a_w_out", [2, DRNN, D])
    kv_w = din("kv_w", [D, 2 * D + NH])
    b_w_in = din("b_w_in", [2, D, 2 * D])
    b_w_out = din("b_w_out", [2, D, D])
    f_w_in = din("f_w_in", [4, D, 2 * DFF])
    f_w_out = din("f_w_out", [4, DFF, D])
    pa_d = din("pa", [BLK, 2 * NBLK * 8])
    pf_d = din("pf", [128, 4 * 44 * 4])
    pl_d = din("pl", [128, 4 * 8 * 4])
    fb_d = din("fb", [NH, 1])
    outT = nc.dram_tensor("outT", [D, SEQ], F32, kind="ExternalOutput").ap()
    kaug = nc.dram_tensor("kaug", [NH, KAUG, L], BF16).ap()
    qc = nc.dram_tensor("qc", [NH, 6, L], BF16).ap()
    vaug = nc.dram_tensor("vaug", [NH, 128, NTT, DH + 1], BF16).ap()

    st = ExitStack()
    P = Prog(nc, st)

    h = P.sb([128, NK, L], F32)
    hb = P.sb([128, NK, PAD + L], BF16)
    wall = P.sb([128, 2 * WH], BF16)
    wsm = [P.sb([128, 1024], BF16) for _ in range(2)]
    slots = [P.sb([128, SW], F32) for _ in range(NSLOT)]
    obuf = [P.sb([128, 4 * SW], BF16) for _ in range(2)]
    kbuf = [P.sb([128, L], BF16) for _ in range(2)]
    qbuf = [P.sb([128, L], BF16) for _ in range(2)]
    vbuf = [P.sb([128, NTT * (DH + 1)], BF16) for _ in range(2)]
    pa_t = P.sb([BLK, 2 * NBLK * 8], F32)
    pf_t = P.sb([128, 4 * 44 * 4], F32)
    pl_t = P.sb([128, 4 * 8 * 4], F32)
    pd_t = P.sb([BLK, 6 * 2 * NBLK], F32)
    carry = P.sb([BLK, NBLK], F32)
    fb_t = P.sb([NH, 4], F32)
    cst = P.sb([128, SW], F32)
    cst2 = P.sb([128, SW], F32)
    onesf = P.sb([128, 64], F32)
    onesm = P.sb([128, 128], BF16)
    ident = P.sb([128, 128], BF16)
    maskT = P.sb([128, 128], BF16)
    zer = P.sb([128, 128], F32)
    one1 = P.sb([128, 128], F32)
    banks = [P.ps([128, 512]) for _ in range(8)]

    pa_v = pa_t[:, :].rearrange("c (l n e) -> c l n e", l=2, n=NBLK)
    pf_v = pf_t[:, :].rearrange("p (l j e) -> p l j e", l=4, j=44)
    pl_v = pl_t[:, :].rearrange("p (l k e) -> p l k e", l=4, k=NK)
    pd_v = pd_t[:, :].rearrange("c (q l n) -> c q l n", q=6, l=2)

    st_ = {"bank": 0, "slot": 0, "job": 0, "pb": 0, "sbk": 0, "ob": 0, "pjb": 0, "pt": 0}

    def bank(pool=None):
        if pool is None:
            i = st_["bank"] % 8
            st_["bank"] += 1
        elif pool == "S":
            i = st_["sbk"] % 4
            st_["sbk"] += 1
        elif pool == "J":
            i = 4 + st_["pjb"] % 2
            st_["pjb"] += 1
        else:
            i = 6 + st_["ob"] % 2
            st_["ob"] += 1
        return banks[i], ("bank", i)

    def slot():
        i = st_["slot"] % NSLOT
        st_["slot"] += 1
        return slots[i], ("slot", i)

    def chunks_of(t0, t1):
        t0 = max(t0, 0)
        return range(t0 // N, min((t1 - 1) // N, NCH - 1) + 1)

    def hbkeys(t0, t1):
        return [("hb", k, c) for k in range(NK) for c in chunks_of(t0, t1)]

    def hkeys(t0, t1, k=None):
        ks = range(NK) if k is None else [k]
        return [("h", kk, c) for kk in ks for c in chunks_of(t0, t1)]

    def wkeys(half):
        return [("w", u) for u in range(6 * half, 6 * half + 6)]

    ld = P.dsem("ld")
    wsem = [P.dsem("w0"), P.dsem("w1")]
    wssem = [P.dsem("ws0"), P.dsem("ws1")]
    ksem = [P.dsem("k0"), P.dsem("k1")]
    stsem = P.dsem("st")
    outsem = P.dsem("out")

    def dma(eng, out_ap, in_ap, reads, writes, dsem):
        return P.op(eng, I("dma_start", out=out_ap, in_=in_ap), reads, writes, dsem)

    P.op("vector", I("memset", cst[:, :], -0.5), writes=["cst"])
    P.op("vector", I("memset", cst2[:, :], 0.5), writes=["cst2"])
    P.op("vector", I("memset", onesf[:, :], 1.0), writes=["onesf"])
    P.op("vector", I("memset", onesm[:, :], 1.0 / 1024.0), writes=["onesm"])
    P.op("vector", I("memset", zer[:, :], 0.0), writes=["zer"])
    P.op("vector", I("memset", one1[:, :], 1.0), writes=["one1"])
    P.op("vector", I("memset", carry[:, :], 0.0), writes=["carry"])
    for k in range(NK):
        P.op("gpsimd", I("memset", hb[:, k, 0:PAD], 0.0), writes=[("hbpad", k)])
    P.op("gpsimd", I("affine_select", out=ident[:, :], in_=one1[:, :], pattern=[[1, 128]],
                                             compare_op=ALU.is_equal, fill=0.0, base=0, channel_multiplier=-1),
         reads=["one1"], writes=["ident"])
    P.op("gpsimd", I("affine_select", out=maskT[:, :], in_=zer[:, :], pattern=[[1, 128]],
                                             compare_op=ALU.is_ge, fill=-30000.0, base=0, channel_multiplier=-1),
         reads=["zer"], writes=["maskT"])
    dma("sync", pa_t[:, :], pa_d, [], ["pa"], ld)
    dma("sync", pf_t[:, :], pf_d, [], ["pf"], ld)
    dma("sync", pl_t[:, :], pl_d, [], ["pl"], ld)
    dma("sync", fb_t[:, 0:1], fb_d, [], ["fb"], ld)
    dma("sync", h[:, :, 0:NMETA], metaT.rearrange("(k p) t -> p k t", p=128), [], ["hmeta"], ld)
    for k in range(NK):
        dma("sync", h[:, k, NMETA:L], xT[k * 128:(k + 1) * 128, :], ["hmeta"], hkeys(0, L, k), ld)
    for k in range(NK):
        for c in range(NCH):
            eng = ["vector", "gpsimd", "scalar"][(k * NCH + c) % 3]
            t0 = c * N
            if eng == "scalar":
                P.op(eng, I("activation", out=hb[:, k, PAD + t0:PAD + t0 + N], in_=h[:, k, t0:t0 + N],
                                                             func=AF.Copy),
                     reads=[("h", k, c)], writes=[("hb", k, c)])
            else:
                P.op(eng, I("tensor_copy", out=hb[:, k, PAD + t0:PAD + t0 + N], in_=h[:, k, t0:t0 + N]),
                     reads=[("h", k, c)], writes=[("hb", k, c)])
    lam_v = pa_v[:, :, :, 7]
    P.op("scalar", I("activation", out=pd_v[:, 4], in_=lam_v, func=AF.Exp, scale=-1.0), reads=["pa"], writes=["pd4"])
    P.op("scalar", I("activation", out=pd_v[:, 5], in_=pd_v[:, 4], func=AF.Ln, bias=1.0), reads=["pd4"], writes=["pd5"])
    P.op("vector", I("tensor_scalar", out=pd_v[:, 0], in0=pd_v[:, 5], scalar1=-4.0, scalar2=None, op0=ALU.mult),
         reads=["pd5"], writes=["pd"])
    P.op("vector", I("tensor_scalar", out=pd_v[:, 1], in0=pd_v[:, 5], scalar1=4.0, scalar2=None, op0=ALU.mult),
         reads=["pd5"], writes=["pd"])
    P.op("vector", I("tensor_scalar", out=pd_v[:, 2], in0=pa_v[:, :, :, 5], scalar1=0.5, scalar2=None, op0=ALU.mult),
         reads=["pa"], writes=["pd"])
    P.op("vector", I("tensor_scalar", out=pd_v[:, 3], in0=pa_v[:, :, :, 6], scalar1=0.5, scalar2=None, op0=ALU.mult),
         reads=["pa"], writes=["pd"])
    P.op("vector", I("tensor_scalar", out=fb_t[:, 1:2], in0=fb_t[:, 0:1], scalar1=-1.0, scalar2=None, op0=ALU.mult),
         reads=["fb"], writes=["nfb"])

    def layer_norm(l, which):
        gi, bi = 2 * which, 2 * which + 1
        for c in range(NCH):
            t0 = c * N
            b1, b1k = bank()
            b2, b2k = bank()
            for half in range(2):
                for kk in range(4):
                    k = half * 4 + kk
                    P.op("gpsimd", I("tensor_copy", out=obuf[0][:, kk * SW:kk * SW + N], in_=h[:, k, t0:t0 + N]),
                         reads=[("h", k, c)], writes=[("ob", 0, kk)])
                    P.op("scalar", I("activation", out=obuf[1][:, kk * SW:kk * SW + N], in_=h[:, k, t0:t0 + N],
                                                                      func=AF.Square),
                         reads=[("h", k, c)], writes=[("ob", 1, kk)])
                P.op("tensor", mm(b1[:, 0:N], [(onesm[:, :], obuf[0][:, kk * SW:kk * SW + N]) for kk in range(4)],
                                  start=(half == 0), stop=(half == 1)),
                     reads=["onesm"] + [("ob", 0, kk) for kk in range(4)], writes=[b1k])
                P.op("tensor", mm(b2[:, 0:N], [(onesm[:, :], obuf[1][:, kk * SW:kk * SW + N]) for kk in range(4)],
                                  start=(half == 0), stop=(half == 1)),
                     reads=["onesm"] + [("ob", 1, kk) for kk in range(4)], writes=[b2k])
            mean, mk = slot()
            msq, qk = slot()
            vpe, vk = slot()
            P.op("vector", I("tensor_copy", out=mean[:, 0:N], in_=b1[:, 0:N]), reads=[b1k], writes=[mk])
            P.op("gpsimd", I("tensor_tensor", out=msq[:, 0:N], in0=mean[:, 0:N], in1=mean[:, 0:N], op=ALU.mult),
                 reads=[mk], writes=[qk])
            P.op("vector", I("scalar_tensor_tensor", out=vpe[:, 0:N], in0=b2[:, 0:N], scalar=EPS, in1=msq[:, 0:N],
                                                            op0=ALU.add, op1=ALU.subtract),
                 reads=[b2k, qk], writes=[vk])
            P.op("gpsimd", I("tensor_tensor", out=vpe[:, 0:N], in0=vpe[:, 0:N], in1=cst[:, 0:N], op=ALU.pow),
                 reads=[vk, "cst"], writes=[vk])
            for k in range(NK):
                t1, tk = slot()
                P.op("vector", I("tensor_tensor", out=t1[:, 0:N], in0=h[:, k, t0:t0 + N], in1=mean[:, 0:N],
                                                                     op=ALU.subtract),
                     reads=[("h", k, c), mk], writes=[tk])
                P.op("vector", I("tensor_tensor", out=t1[:, 0:N], in0=t1[:, 0:N], in1=vpe[:, 0:N], op=ALU.mult),
                     reads=[tk, vk], writes=[tk])
                P.op("scalar", I("activation", out=h[:, k, t0:t0 + N], in_=t1[:, 0:N], func=AF.Identity,
                                                                  scale=pl_v[:, l, k, gi:gi + 1], bias=pl_v[:, l, k, bi:bi + 1]),
                     reads=[tk, "pl"], writes=[("h", k, c)])
                P.op("gpsimd", I("tensor_scalar", out=hb[:, k, PAD + t0:PAD + t0 + N], in0=t1[:, 0:N],
                                                                     scalar1=pl_v[:, l, k, gi:gi + 1], scalar2=pl_v[:, l, k, bi:bi + 1],
                                                                     op0=ALU.mult, op1=ALU.add),
                     reads=[tk, "pl"], writes=[("hb", k, c)])

    def accum(m, c, pb, pbk, first):
        t0 = c * N
        if first:
            P.op("vector", I("scalar_tensor_tensor", out=h[:, m, t0:t0 + N], in0=h[:, m, t0:t0 + N], scalar=ALPHA,
                                                            in1=pb[:, 0:N], op0=ALU.mult, op1=ALU.add),
                 reads=[("h", m, c), pbk], writes=[("h", m, c)])
        else:
            P.op("vector", I("tensor_tensor", out=h[:, m, t0:t0 + N], in0=h[:, m, t0:t0 + N], in1=pb[:, 0:N], op=ALU.add),
                 reads=[("h", m, c), pbk], writes=[("h", m, c)])

    jobs = []

    def a_load(l, s, half):
        base = half * WH
        wk_ = wkeys(half)
        wg = wall[:, base:base + 3072].rearrange("p (k c) -> p k c", k=NK)
        wr = wall[:, base + 3072:base + 6144].rearrange("p (k c) -> p k c", k=NK)
        gr = wall[0:BLK, base + 6144:base + 6528].rearrange("c (n d) -> c n d", n=4)
        gi_ = wall[0:BLK, base + 6528:base + 6912].rearrange("c (n d) -> c n d", n=4)
        wo = wall[0:BLK, base + 6912:base + 6912 + 4096].rearrange("c (n m) -> c n m", n=4)
        ds = wsem[half]
        dma("gpsimd", wg, a_w_in[l, :, 384 * s:384 * s + 384].rearrange("(k p) c -> p k c", p=128), [], wk_, ds)
        dma("gpsimd", wr, a_w_in[l, :, DRNN + 384 * s:DRNN + 384 * s + 384].rearrange("(k p) c -> p k c", p=128), [], wk_, ds)
        dma("gpsimd", gr, a_w_r[l, 4 * s:4 * s + 4].rearrange("n c d -> c n d"), [], wk_, ds)
        dma("gpsimd", gi_, a_w_i[l, 4 * s:4 * s + 4].rearrange("n c d -> c n d"), [], wk_, ds)
        dma("gpsimd", wo, a_w_out[l, 384 * s:384 * s + 384, :].rearrange("(n c) m -> c n m", c=BLK), [], wk_, ds)

    def a_compute(l, s, half):
        base = half * WH
        wk_ = wkeys(half)
        wg = wall[:, base:base + 3072].rearrange("p (k c) -> p k c", k=NK)
        wr = wall[:, base + 3072:base + 6144].rearrange("p (k c) -> p k c", k=NK)
        gr = wall[0:BLK, base + 6144:base + 6528].rearrange("c (n d) -> c n d", n=4)
        gi_ = wall[0:BLK, base + 6528:base + 6912].rearrange("c (n d) -> c n d", n=4)
        wo = wall[0:BLK, base + 6912:base + 6912 + 4096].rearrange("c (n m) -> c n m", n=4)
        for c in range(NCH):
            t0 = c * N
            obi = st_["pb"] % 2
            st_["pb"] += 1
            og = obuf[obi]
            for nb in range(4):
                n = 4 * s + nb
                cs = slice(nb * BLK, (nb + 1) * BLK)
                pg, pgk = bank()
                pr, prk = bank()
                P.op("tensor", mm(pg[0:BLK, 0:N], [(wg[:, k, cs], hb[:, k, PAD + t0:PAD + t0 + N]) for k in range(NK)]),
                     reads=wk_ + hbkeys(t0, t0 + N), writes=[pgk])
                P.op("tensor", mm(pr[0:BLK, 0:N + 3], [(wr[:, k, cs], hb[:, k, t0:t0 + N + 3]) for k in range(NK)]),
                     reads=wk_ + hbkeys(t0 - 3, t0 + N) + [("hbpad", k) for k in range(NK)], writes=[prk])
                gg, ggk = slot()
                xc, xck = slot()
                tr, trk = slot()
                ti, tik = slot()
                aa, aak = slot()
                tl, tlk = slot()
                hs_, hsk = slot()
                xcb = kbuf[nb % 2]
                xcbk = ("kbuf", nb % 2)
                cw = lambda j: pa_v[:, l, n, j:j + 1]
                P.op("scalar", I("activation", out=xc[0:BLK, 0:N], in_=pr[0:BLK, 3:N + 3], func=AF.Identity,
                                                      scale=cw(3), bias=cw(4)), reads=[prk, "pa"], writes=[xck])
                P.op("scalar", I("activation", out=gg[0:BLK, 0:N], in_=pg[0:BLK, 0:N], func=AF.Gelu_apprx_tanh),
                     reads=[pgk], writes=[ggk])
                for j in (2, 1, 0):
                    P.op("vector", I("scalar_tensor_tensor", out=xc[0:BLK, 0:N], in0=pr[0:BLK, j:j + N], scalar=cw(j),
                                                                         in1=xc[0:BLK, 0:N], op0=ALU.mult, op1=ALU.add),
                         reads=[prk, xck, "pa"], writes=[xck])
                P.op("gpsimd", I("tensor_copy", out=xcb[0:BLK, 0:N], in_=xc[0:BLK, 0:N]), reads=[xck], writes=[xcbk])
                p1, p1k = bank()
                p2, p2k = bank()
                P.op("tensor", mm(p1[0:BLK, 0:N], [(gr[:, nb, :], xcb[0:BLK, 0:N])]), reads=wk_ + [xcbk], writes=[p1k])
                P.op("tensor", mm(p2[0:BLK, 0:N], [(gi_[:, nb, :], xcb[0:BLK, 0:N])]), reads=wk_ + [xcbk], writes=[p2k])
                P.op("scalar", I("activation", out=tr[0:BLK, 0:N], in_=p1[0:BLK, 0:N], func=AF.Tanh, scale=0.5,
                                                      bias=pd_v[:, 2, l, n:n + 1]), reads=[p1k, "pd"], writes=[trk])
                P.op("scalar", I("activation", out=ti[0:BLK, 0:N], in_=p2[0:BLK, 0:N], func=AF.Tanh, scale=0.5,
                                                      bias=pd_v[:, 3, l, n:n + 1]), reads=[p2k, "pd"], writes=[tik])
                P.op("scalar", I("activation", out=aa[0:BLK, 0:N], in_=tr[0:BLK, 0:N], func=AF.Exp,
                                                      scale=pd_v[:, 0, l, n:n + 1], bias=pd_v[:, 0, l, n:n + 1]),
                     reads=[trk, "pd"], writes=[aak])
                P.op("scalar", I("activation", out=tl[0:BLK, 0:N], in_=tr[0:BLK, 0:N], func=AF.Tanh,
                                                      scale=pd_v[:, 1, l, n:n + 1], bias=pd_v[:, 1, l, n:n + 1]),
                     reads=[trk, "pd"], writes=[tlk])
                P.op("scalar", I("activation", out=tr[0:BLK, 0:N], in_=aa[0:BLK, 0:N], func=AF.Square),
                     reads=[aak, trk], writes=[trk])
                P.op("vector", I("scalar_tensor_tensor", out=ti[0:BLK, 0:N], in0=ti[0:BLK, 0:N], scalar=1.0,
                                                                in1=xc[0:BLK, 0:N], op0=ALU.add, op1=ALU.mult),
                     reads=[tik, xck], writes=[tik])
                P.op("vector", I("scalar_tensor_tensor", out=tl[0:BLK, 0:N], in0=tr[0:BLK, 0:N], scalar=1.0,
                                                                in1=tl[0:BLK, 0:N], op0=ALU.add, op1=ALU.mult),
                     reads=[trk, tlk], writes=[tlk])
                P.op("gpsimd", I("tensor_tensor", out=tl[0:BLK, 0:N], in0=tl[0:BLK, 0:N], in1=cst2[0:BLK, 0:N], op=ALU.pow),
                     reads=[tlk, "cst2"], writes=[tlk])
                P.op("gpsimd", I("tensor_tensor", out=ti[0:BLK, 0:N], in0=ti[0:BLK, 0:N], in1=tl[0:BLK, 0:N], op=ALU.mult),
                     reads=[tik, tlk], writes=[tik])
                ck = ("carry", n)
                P.op("vector", I("tensor_tensor_scan", out=hs_[0:BLK, 0:N], data0=aa[0:BLK, 0:N], data1=ti[0:BLK, 0:N],
                                                              initial=carry[:, n:n + 1], op0=ALU.mult, op1=ALU.add),
                     reads=[aak, tik, ck, "carry"], writes=[hsk])
                P.op("gpsimd", I("tensor_copy", out=carry[:, n:n + 1], in_=hs_[0:BLK, N - 1:N]), reads=[hsk], writes=[ck])
                P.op("vector", I("scalar_tensor_tensor", out=og[0:BLK, nb * SW:nb * SW + N], in0=hs_[0:BLK, 0:N], scalar=0.5,
                                                                in1=gg[0:BLK, 0:N], op0=ALU.mult, op1=ALU.mult),
                     reads=[hsk, ggk], writes=[("ob", obi, nb)])
            for m in range(NK):
                po, pok = bank()
                P.op("tensor", mm(po[:, 0:N], [(wo[:, nb, m * 128:(m + 1) * 128], og[0:BLK, nb * SW:nb * SW + N]) for nb in range(4)]),
                     reads=wk_ + [("ob", obi, nb) for nb in range(4)], writes=[pok])
                accum(m, c, po, pok, first=(s == 0))
        if s == 3:
            P.op("vector", I("memset", carry[:, :], 0.0), reads=[("carry", n) for n in range(NBLK)],
                 writes=["carry"] + [("carry", n) for n in range(NBLK)])
            layer_norm(l, 0)

    def f_load(l, si, half):
        J = FFN_SLICES[si]
        kk = len(J)
        base = half * WH
        wk_ = wkeys(half)
        ds = wsem[half]
        wfg = wall[:, base:base + NK * kk * 128].rearrange("p (k c) -> p k c", k=NK)
        wfv = wall[:, base + 4096:base + 4096 + NK * kk * 128].rearrange("p (k c) -> p k c", k=NK)
        wfo = wall[:, base + 8192:base + 8192 + kk * 1024].rearrange("p (j m) -> p j m", j=kk)
        j0 = J[0]
        dma("gpsimd", wfg, f_w_in[l, :, 128 * j0:128 * (j0 + kk)].rearrange("(k p) c -> p k c", p=128), [], wk_, ds)
        dma("gpsimd", wfv, f_w_in[l, :, DFF + 128 * j0:DFF + 128 * (j0 + kk)].rearrange("(k p) c -> p k c", p=128), [], wk_, ds)
        dma("gpsimd", wfo, f_w_out[l, 128 * j0:128 * (j0 + kk), :].rearrange("(j p) m -> p j m", p=128), [], wk_, ds)

    def f_compute(l, si, half):
        J = FFN_SLICES[si]
        kk = len(J)
        base = half * WH
        wk_ = wkeys(half)
        wfg = wall[:, base:base + NK * kk * 128].rearrange("p (k c) -> p k c", k=NK)
        wfv = wall[:, base + 4096:base + 4096 + NK * kk * 128].rearrange("p (k c) -> p k c", k=NK)
        wfo = wall[:, base + 8192:base + 8192 + kk * 1024].rearrange("p (j m) -> p j m", j=kk)
        for c in range(NCH):
            t0 = c * N
            obi = st_["pb"] % 2
            st_["pb"] += 1
            pr_ = obuf[obi]
            for jj, j in enumerate(J):
                cs = slice(jj * 128, (jj + 1) * 128)
                pg, pgk = bank()
                pv, pvk = bank()
                rk = wk_ + hbkeys(t0 - 2, t0 + N) + [("hbpad", k) for k in range(NK)]
                P.op("tensor", mm(pg[:, 0:N + 2], [(wfg[:, k, cs], hb[:, k, t0 + 1:t0 + N + 3]) for k in range(NK)]), reads=rk, writes=[pgk])
                P.op("tensor", mm(pv[:, 0:N + 2], [(wfv[:, k, cs], hb[:, k, t0 + 1:t0 + N + 3]) for k in range(NK)]), reads=rk, writes=[pvk])
                gc, gck = slot()
                vc, vck = slot()
                fw = lambda jx, e_: pf_v[:, l, jx, e_:e_ + 1]
                jv = NPAIR + j
                P.op("scalar", I("activation", out=gc[:, 0:N], in_=pg[:, 2:N + 2], func=AF.Identity, scale=fw(j, 2), bias=fw(j, 3)),
                     reads=[pgk, "pf"], writes=[gck])
                P.op("scalar", I("activation", out=vc[:, 0:N], in_=pv[:, 2:N + 2], func=AF.Identity, scale=fw(jv, 2), bias=fw(jv, 3)),
                     reads=[pvk, "pf"], writes=[vck])
                for tap in (1, 0):
                    P.op("vector", I("scalar_tensor_tensor", out=gc[:, 0:N], in0=pg[:, tap:tap + N], scalar=fw(j, tap),
                                                                             in1=gc[:, 0:N], op0=ALU.mult, op1=ALU.add),
                         reads=[pgk, gck, "pf"], writes=[gck])
                    P.op("vector", I("scalar_tensor_tensor", out=vc[:, 0:N], in0=pv[:, tap:tap + N], scalar=fw(jv, tap),
                                                                             in1=vc[:, 0:N], op0=ALU.mult, op1=ALU.add),
                         reads=[pvk, vck, "pf"], writes=[vck])
                P.op("scalar", I("activation", out=gc[:, 0:N], in_=gc[:, 0:N], func=AF.Gelu_apprx_tanh), reads=[gck], writes=[gck])
                P.op("gpsimd", I("tensor_tensor", out=pr_[:, jj * SW:jj * SW + N], in0=gc[:, 0:N], in1=vc[:, 0:N], op=ALU.mult),
                     reads=[gck, vck], writes=[("ob", obi, jj)])
            for m in range(NK):
                po, pok = bank()
                P.op("tensor", mm(po[:, 0:N], [(wfo[:, jj, m * 128:(m + 1) * 128], pr_[:, jj * SW:jj * SW + N]) for jj in range(kk)]),
                     reads=wk_ + [("ob", obi, jj) for jj in range(kk)], writes=[pok])
                accum(m, c, po, pok, first=(si == 0))
        if si == len(FFN_SLICES) - 1:
            layer_norm(l, 1)

    def k_load(half):
        base = 0
        wk_ = wkeys(0)
        wk = wall[:, base:base + 8192].rearrange("p (k c) -> p k c", k=NK)
        wf = wall[:, base + 8192:base + 8192 + 128].rearrange("p (k c) -> p k c", k=NK)
        dma("gpsimd", wk, kv_w[:, 0:D].rearrange("(k p) c -> p k c", p=128), [], wk_, wsem[0])
        dma("gpsimd", wf, kv_w[:, 2 * D:2 * D + NH].rearrange("(k p) c -> p k c", p=128), [], wk_, wsem[0])

    def k_compute(half):
        wk_ = wkeys(0)
        wk = wall[:, 0:8192].rearrange("p (k c) -> p k c", k=NK)
        wf = wall[:, 8192:8192 + 128].rearrange("p (k c) -> p k c", k=NK)
        onesrow = wall[0:NH, 11 * UNIT:12 * UNIT]
        P.op("vector", I("memset", onesrow, 1.0), writes=[("w", 11)])
        for j in range(3):
            dma("sync", kaug[:, 64 + j, :], onesrow, [("w", 11)], [("kaug_c", j)], stsem)
        P.op("vector", I("memset", onesrow, -1.0), writes=[("w", 11)])
        for j in range(3):
            dma("sync", qc[:, 3 + j, :], onesrow, [("w", 11)], [("qc_c", 3 + j)], stsem)
        P.op("vector", I("memset", fb_t[:, 2:3], 0.0), writes=["ccarry"])
        for c in range(NCH):
            t0 = c * N
            pz, pzk = bank()
            P.op("tensor", mm(pz[0:NH, 0:N], [(wf[:, k, :], hb[:, k, PAD + t0:PAD + t0 + N]) for k in range(NK)]),
                 reads=wk_ + hbkeys(t0, t0 + N), writes=[pzk])
            e1, e1k = slot()
            cc, cck = slot()
            r1, r1k = slot()
            P.op("scalar", I("activation", out=e1[0:NH, 0:N], in_=pz[0:NH, 0:N], func=AF.Exp, scale=-1.0, bias=fb_t[:, 1:2]),
                 reads=[pzk, "nfb"], writes=[e1k])
            P.op("scalar", I("activation", out=e1[0:NH, 0:N], in_=e1[0:NH, 0:N], func=AF.Ln, bias=1.0), reads=[e1k], writes=[e1k])
            on, onk = slot()
            P.op("gpsimd", I("memset", on[0:NH, 0:N], 1.0), writes=[onk])
            P.op("vector", I("tensor_tensor_scan", out=cc[0:NH, 0:N], data0=on[0:NH, 0:N], data1=e1[0:NH, 0:N],
                                                          initial=fb_t[:, 2:3], op0=ALU.mult, op1=ALU.subtract),
                 reads=[e1k, onk, "ccarry"], writes=[cck])
            P.op("gpsimd", I("tensor_copy", out=fb_t[:, 2:3], in_=cc[0:NH, N - 1:N]), reads=[cck], writes=["ccarry"])
            ob = obuf[c % 2]
            hi = ob[0:NH, 0:N]
            lo = ob[0:NH, SW:SW + N]
            ll = ob[0:NH, 2 * SW:2 * SW + N]
            okk = [("ob", c % 2, j) for j in range(3)]
            P.op("vector", I("tensor_copy", out=hi, in_=cc[0:NH, 0:N]), reads=[cck], writes=[okk[0]])
            P.op("vector", I("tensor_tensor", out=r1[0:NH, 0:N], in0=cc[0:NH, 0:N], in1=hi, op=ALU.subtract),
                 reads=[cck, okk[0]], writes=[r1k])
            P.op("vector", I("tensor_copy", out=lo, in_=r1[0:NH, 0:N]), reads=[r1k], writes=[okk[1]])
            P.op("vector", I("tensor_tensor", out=r1[0:NH, 0:N], in0=r1[0:NH, 0:N], in1=lo, op=ALU.subtract),
                 reads=[r1k, okk[1]], writes=[r1k])
            P.op("vector", I("tensor_copy", out=ll, in_=r1[0:NH, 0:N]), reads=[r1k], writes=[okk[2]])
            for j, src in enumerate((hi, lo, ll)):
                dma("sync", kaug[:, 67 + j, t0:t0 + N], src, [okk[j]], [("kaug_c", 3 + j, c)], stsem)
                dma("sync", qc[:, j, t0:t0 + N], src, [okk[j]], [("qc_c", j, c)], stsem)
        for hd in range(NH):
            kb = hd % 2
            for sb in range(5):
                q0 = 512 * sb
                q1 = min(q0 + 512, L)
                nq = q1 - q0
                pk, pkk = bank()
                P.op("tensor", mm(pk[0:DH, 0:nq], [(wk[:, k, hd * DH:(hd + 1) * DH], hb[:, k, PAD + q0:PAD + q1]) for k in range(NK)]),
                     reads=wk_ + hbkeys(q0, q1), writes=[pkk])
                if (hd * 5 + sb) % 2 == 0:
                    P.op("scalar", I("activation", out=qbuf[kb][0:DH, q0:q1], in_=pk[0:DH, 0:nq],
                                                                                            func=AF.Copy),
                         reads=[pkk], writes=[("qbuf", kb, sb)])
                else:
                    P.op("vector", I("tensor_copy", out=qbuf[kb][0:DH, q0:q1], in_=pk[0:DH, 0:nq]),
                         reads=[pkk], writes=[("qbuf", kb, sb)])
            dma("sync", kaug[hd, 0:DH, :], qbuf[kb][0:DH, :], [("qbuf", kb, sb) for sb in range(5)], [("kaug", hd)], stsem)

    def v_load(half):
        wk_ = [("w", u) for u in range(6, 10)]
        wv = wall[:, WH:WH + 8192].rearrange("p (k c) -> p k c", k=NK)
        dma("gpsimd", wv, kv_w[:, D:2 * D].rearrange("(k p) c -> p k c", p=128), [], wk_, wsem[1])

    def v_compute(half):
        wk_ = [("w", u) for u in range(6, 10)]
        wv = wall[:, WH:WH + 8192].rearrange("p (k c) -> p k c", k=NK)
        for i in range(2):
            v3 = obuf[i][:, 0:NH * (DH + 1)].rearrange("p (a b) -> p a b", a=NH)
            P.op("vector", I("memset", v3[:, :, DH:DH + 1], 1.0), writes=[("ob", i, j) for j in range(4)])
        for tt in range(NTT):
            tok0 = 128 * tt
            nt = min(128, L - tok0)
            i = tt % 2
            v3 = obuf[i][:, 0:NH * (DH + 1)].rearrange("p (a b) -> p a b", a=NH)
            okk = [("ob", i, j) for j in range(4)]
            for jv in range(2):
                pv, pvk = bank()
                P.op("tensor", mm(pv[0:nt, 0:512], [(hb[:, k, PAD + tok0:PAD + tok0 + nt], wv[:, k, 512 * jv:512 * jv + 512]) for k in range(NK)]),
                     reads=wk_ + hbkeys(tok0, tok0 + nt), writes=[pvk])
                src = pv[0:nt, 0:512].rearrange("p (a b) -> p a b", a=8)
                dst = v3[0:nt, 8 * jv:8 * jv + 8, 0:DH]
                if jv == 0:
                    P.op("scalar", I("activation", out=dst, in_=src, func=AF.Copy), reads=[pvk] + okk, writes=okk)
                else:
                    P.op("vector", I("tensor_copy", out=dst, in_=src), reads=[pvk] + okk, writes=okk)
            dma("sync", vaug[:, 0:nt, tt, :].rearrange("h p e -> p h e"), v3[0:nt, :, :], okk, [("vaug", tt)], stsem)

    def b_load(j, half):
        wbo = wall[:, 8 * UNIT:8 * UNIT + 8192].rearrange("p (k m) -> p k m", k=NK)
        dma("gpsimd", wbo, b_w_out[j].rearrange("(k p) m -> p k m", p=128), [], [("w", u) for u in range(8, 12)], wsem[1])

    def b_compute(l, j, half):
        wbo = wall[:, 8 * UNIT:8 * UNIT + 8192].rearrange("p (k m) -> p k m", k=NK)
        kvdeps = ([("kaug_c", jj) for jj in range(3)] + [("qc_c", 3 + jj) for jj in range(3)]
                  + [("kaug_c", 3 + jj, c) for jj in range(3) for c in range(NCH)]
                  + [("qc_c", jj, c) for jj in range(3) for c in range(NCH)]
                  + [("vaug", tt) for tt in range(NTT)])
        for hd in range(NH):
            kb = hd % 2
            w3 = wsm[kb][:, :].rearrange("p (k c) -> p k c", k=NK)
            dma("sync", kbuf[kb][0:KAUG, :], kaug[hd], kvdeps + [("kaug", hd)], [("kbuf", kb)], ksem[kb])
            dma("sync", vbuf[kb][:, :], vaug[hd].rearrange("p t e -> p (t e)"), kvdeps, [("vbuf", kb)], ksem[kb])
            dma("sync", qbuf[kb][DH:KAUG, :], qc[hd], kvdeps, [("qbufc", kb)], ksem[kb])
            dma("gpsimd", w3[:, :, 0:DH], b_w_in[j, :, hd * DH:(hd + 1) * DH].rearrange("(k p) c -> p k c", p=128), [], [("wsm", kb)], wssem[kb])
            dma("gpsimd", w3[:, :, DH:2 * DH], b_w_in[j, :, D + hd * DH:D + (hd + 1) * DH].rearrange("(k p) c -> p k c", p=128), [],
                [("wsm", kb)], wssem[kb])
            for sb in range(5):
                q0 = 512 * sb
                q1 = min(q0 + 512, L)
                nq = q1 - q0
                rk = [("wsm", kb)] + hbkeys(q0, q1)
                pq, pqk = bank("J")
                P.op("tensor", mm(pq[0:DH, 0:nq], [(w3[:, k, 0:DH], hb[:, k, PAD + q0:PAD + q1]) for k in range(NK)]), reads=rk, writes=[pqk])
                P.op("scalar", I("activation", out=qbuf[kb][0:DH, q0:q1], in_=pq[0:DH, 0:nq],
                                                                                func=AF.Identity, scale=0.125),
                     reads=[pqk], writes=[("qbuf", kb, sb)])
                pgt, pgtk = bank("J")
                P.op("tensor", mm(pgt[0:DH, 0:nq], [(w3[:, k, DH:2 * DH], hb[:, k, PAD + q0:PAD + q1]) for k in range(NK)]), reads=rk, writes=[pgtk])
                pO, pOk = bank("O")
                nkc = (q1 + 127) // 128
                for kc in range(nkc):
                    k0 = 128 * kc
                    ks = min(128, L - k0)
                    off = max(k0 - q0, 0)
                    diag = k0 >= q0
                    pS, pSk = bank("S")

                    def smm(e, pS=pS, k0=k0, ks=ks, off=off, diag=diag, kb=kb, q0=q0, q1=q1, nq=nq):
                        ins = e.matmul(pS[0:ks, off:nq], lhsT=kbuf[kb][0:KAUG, k0:k0 + ks], rhs=qbuf[kb][0:KAUG, q0 + off:q1],
                                       start=True, stop=not diag, skip_group_check=True)
                        if diag:
                            w_ = min(128, nq - off)
                            ins = e.matmul(pS[0:ks, off:off + w_], lhsT=ident[0:ks, 0:ks], rhs=maskT[0:ks, 0:w_],
                                           start=False, stop=True, skip_group_check=True)
                        return ins
                    P.op("tensor", smm, reads=[("kbuf", kb), ("qbuf", kb, sb), ("qbufc", kb), "ident", "maskT"], writes=[pSk])
                    pi = st_["pt"] % 4
                    st_["pt"] += 1
                    pT = obuf[pi // 2][:, (pi % 2) * 2 * SW:(pi % 2) * 2 * SW + 512]
                    pTk = [("ob", pi // 2, (pi % 2) * 2), ("ob", pi // 2, (pi % 2) * 2 + 1)]
                    P.op("scalar", I("activation", out=pT[0:ks, off:nq], in_=pS[0:ks, off:nq], func=AF.Exp),
                         reads=[pSk], writes=pTk)
                    P.op("tensor", mm(pO[0:DH + 1, off:nq], [(vbuf[kb][0:ks, kc * (DH + 1):(kc + 1) * (DH + 1)], pT[0:ks, off:nq])],
                                      start=(kc == 0), stop=(kc == nkc - 1)),
                         reads=[("vbuf", kb)] + pTk, writes=[pOk])
                for hh in range(2):
                    c0 = 256 * hh
                    c1 = min(c0 + 256, nq)
                    if c0 >= nq:
                        break
                    w_ = c1 - c0
                    rd, rdk = slot()
                    tg, tgk = slot()
                    tt_, ttk = slot()
                    P.op("vector", I("reciprocal", out=rd[DH:DH + 1, 0:w_], in_=pO[DH:DH + 1, c0:c1]),
                         reads=[pOk], writes=[rdk])
                    pB, pBk = bank("J")
                    P.op("tensor", mm(pB[0:DH, 0:w_], [(onesf[DH:DH + 1, 0:DH], rd[DH:DH + 1, 0:w_])]), reads=[rdk, "onesf"], writes=[pBk])
                    P.op("scalar", I("activation", out=tg[0:DH, 0:w_], in_=pgt[0:DH, c0:c1], func=AF.Tanh, scale=0.5),
                         reads=[pgtk], writes=[tgk])
                    P.op("vector", I("scalar_tensor_tensor", out=tt_[0:DH, 0:w_], in0=tg[0:DH, 0:w_], scalar=1.0, in1=pO[0:DH, c0:c1], op0=ALU.add, op1=ALU.mult),
                        reads=[tgk, pOk], writes=[ttk])
                    pp = (hd % 2) * DH
                    col = (hd // 2) * UNIT + q0 + c0
                    P.op("vector", I("scalar_tensor_tensor", out=wall[pp:pp + DH, col:col + w_], in0=tt_[0:DH, 0:w_], scalar=0.5, in1=pB[0:DH, 0:w_], op0=ALU.mult, op1=ALU.mult),
                        reads=[ttk, pBk], writes=[("w", hd // 2)])
        for c in range(NCH):
            t0 = c * N
            for m in range(NK):
                po, pok = bank()
                P.op("tensor", mm(po[:, 0:N], [(wbo[:, k, m * 128:(m + 1) * 128], wall[:, k * UNIT + t0:k * UNIT + t0 + N]) for k in range(NK)]),
                     reads=[("w", u) for u in range(12)], writes=[pok])
                accum(m, c, po, pok, first=True)
        layer_norm(l, 0)

    for l in range(nl):
        if l < 2:
            for s in range(4):
                jobs.append((lambda hf, l=l, s=s: a_load(l, s, hf), lambda hf, l=l, s=s: a_compute(l, s, hf), False))
        else:
            if l == 2:
                jobs.append((lambda hf: k_load(hf), lambda hf: k_compute(hf), True))
                jobs.append((lambda hf: v_load(hf), lambda hf: v_compute(hf), True))
            jobs.append((lambda hf, l=l: b_load(l - 2, hf), lambda hf, l=l: b_compute(l, l - 2, hf), True))
        if dbg == "mix" and l == nl - 1:
            break
        for si in range(len(FFN_SLICES)):
            jobs.append((lambda hf, l=l, si=si: f_load(l, si, hf), lambda hf, l=l, si=si: f_compute(l, si, hf), False))

    halves = [i % 2 for i in range(len(jobs))]
    jobs[0][0](halves[0])
    for i, (ldf, cf, ex) in enumerate(jobs):
        nxt = i + 1 < len(jobs)
        pre = nxt and not ex and not jobs[i + 1][2]
        if pre:
            jobs[i + 1][0](halves[i + 1])
        cf(halves[i])
        if nxt and not pre:
            jobs[i + 1][0](halves[i + 1])

    for k in range(NK):
        dma("sync", outT[k * 128:(k + 1) * 128, :], h[:, k, NMETA:L], hkeys(0, L, k), [("out", k)], outsem)
    P.op("sync", lambda e: None, reads=[("out", k) for k in range(NK)])
    counts = P.emit()
    st.close()
    return nc, counts


def host_layout(inp):
    f32 = np.float32
    xT = np.ascontiguousarray(np.transpose(inp["x"], (0, 2, 1)).astype(f32))
    metaT = np.ascontiguousarray(inp["meta"].T.astype(f32))
    cw = np.transpose(inp["a_conv_w"], (0, 2, 1))
    pa = np.concatenate([cw, inp["a_conv_b"][..., None], inp["a_b_r"][..., None], inp["a_b_i"][..., None],
                         inp["a_lambda"][..., None]], axis=-1)
    pa = np.ascontiguousarray(pa.reshape(2, NBLK, BLK, 8).transpose(2, 0, 1, 3)).reshape(BLK, -1).astype(f32)
    pf = np.concatenate([np.transpose(inp["f_conv_w"], (0, 2, 1)), inp["f_conv_b"][..., None]], axis=-1)
    pf = np.ascontiguousarray(pf.reshape(4, 44, 128, 4).transpose(2, 0, 1, 3)).reshape(128, -1).astype(f32)
    pl = np.stack([inp["ln1_g"], inp["ln1_b"], inp["ln2_g"], inp["ln2_b"]], axis=-1)
    pl = np.ascontiguousarray(pl.reshape(4, NK, 128, 4).transpose(2, 0, 1, 3)).reshape(128, -1).astype(f32)
    shared = {
        "metaT": metaT,
        "a_w_in": np.ascontiguousarray(inp["a_w_in"], dtype=f32),
        "a_w_r": np.ascontiguousarray(inp["a_w_r"], dtype=f32),
        "a_w_i": np.ascontiguousarray(inp["a_w_i"], dtype=f32),
        "a_w_out": np.ascontiguousarray(inp["a_w_out"], dtype=f32),
        "kv_w": np.ascontiguousarray(inp["kv_w"], dtype=f32),
        "b_w_in": np.ascontiguousarray(inp["b_w_in"], dtype=f32),
        "b_w_out": np.ascontiguousarray(inp["b_w_out"], dtype=f32),
        "f_w_in": np.ascontiguousarray(inp["f_w_in"], dtype=f32),
        "f_w_out": np.ascontiguousarray(inp["f_w_out"], dtype=f32),
        "pa": pa, "pf": pf, "pl": pl,
        "fb": np.ascontiguousarray(inp["kv_f_b"].reshape(NH, 1), dtype=f32),
    }
    return xT, shared


def kernel(**inp):
    inp = {k: np.asarray(v) for k, v in inp.items()}
    xT, shared = host_layout(inp)
    B = xT.shape[0]
    nc, _ = build(4)
    in_maps = [dict(shared, xT=xT[b]) for b in range(B)]
    res = run_bass_kernel_spmd(nc, in_maps, core_ids=list(range(B)))
    out = np.stack([np.asarray(res.results[b]["outT"]).T for b in range(B)], axis=0)
    return np.ascontiguousarray(out.astype(np.float32))
```

```python
import numpy as np
from contextlib import ExitStack
import concourse.bass as bass
import concourse.mybir as mybir
from concourse.bass_utils import run_bass_kernel_spmd

F32 = mybir.dt.float32
BF16 = mybir.dt.bfloat16
AF = mybir.ActivationFunctionType
ALU = mybir.AluOpType

ENGS = ["tensor", "vector", "scalar", "gpsimd", "sync"]
SEM_LIMIT = 20000

D = 1024
NK = 8
SEQ = 2048
NMETA = 16
L = SEQ + NMETA
PAD = 3
NCH = 6
N = L // NCH
DRNN = 1536
NBLK = 16
BLK = 96
DFF = 2816
NPAIR = 22
NH = 16
DH = 64
ALPHA = 8.0 ** 0.25
EPS = 1e-5
UNIT = 2064
WH = 6 * UNIT
FFN_SLICES = [[0, 1, 2, 3], [4, 5, 6, 7], [8, 9, 10, 11], [12, 13, 14, 15], [16, 17, 18], [19, 20, 21]]
NSLOT = 14
SW = 352
NTT = 17
KAUG = 70


class DSem:
    def __init__(self, name):
        self.name = name
        self.count = 0
        self.h = None


class Op:
    __slots__ = ("eng", "fn", "deps", "needs_signal", "sig", "dsem", "dma_targets")


class Prog:
    def __init__(self, nc, stack):
        self.nc = nc
        self.stack = stack
        self.ops = []
        self.last_writer = {}
        self.readers = {}
        self.dsems = []
        self.nt = 0

    def sb(self, shape, dt):
        self.nt += 1
        return self.stack.enter_context(self.nc.sbuf_tensor(f"sb{self.nt}", list(shape), dt))

    def ps(self, shape, dt=F32):
        self.nt += 1
        return self.stack.enter_context(self.nc.psum_tensor(f"ps{self.nt}", list(shape), dt))

    def dsem(self, name):
        d = DSem(name)
        self.dsems.append(d)
        return d

    def op(self, eng, fn, reads=(), writes=(), dsem=None):
        o = Op()
        o.eng = eng
        o.fn = fn
        o.needs_signal = False
        o.sig = None
        o.dsem = dsem
        deps = set()
        for r in reads:
            w = self.last_writer.get(r)
            if w is not None:
                deps.add(w)
        for w in writes:
            lw = self.last_writer.get(w)
            if lw is not None:
                deps.add(lw)
            for rd in self.readers.get(w, ()):
                deps.add(rd)
        o.dma_targets = {}
        cdeps = []
        for d in deps:
            if d.dsem is not None:
                ds = d.dsem
                o.dma_targets[ds] = max(o.dma_targets.get(ds, 0), ds.count * 16)
            else:
                if d.eng == eng and eng == "tensor":
                    continue
                d.needs_signal = True
                cdeps.append(d)
        o.deps = cdeps
        if dsem is not None:
            dsem.count += 1
        for r in reads:
            self.readers.setdefault(r, []).append(o)
        for w in writes:
            self.last_writer[w] = o
            self.readers[w] = []
        self.ops.append(o)
        return o

    def emit(self):
        nc = self.nc
        nsig = {e: 0 for e in ENGS}
        for o in self.ops:
            if o.dsem is None and o.needs_signal:
                nsig[o.eng] += 1
                o.sig = nsig[o.eng]
        esems = {}
        for e in ENGS:
            n = max(1, (nsig[e] + SEM_LIMIT - 1) // SEM_LIMIT)
            esems[e] = [self.stack.enter_context(nc.semaphore(f"s_{e}_{k}")) for k in range(n)]
        for d in self.dsems:
            d.h = self.stack.enter_context(nc.semaphore(f"d_{d.name}"))
        by_eng = {e: [] for e in ENGS}
        for o in self.ops:
            by_eng[o.eng].append(o)

        def sem_of(e, sig):
            k = (sig - 1) // SEM_LIMIT
            return esems[e][k], (sig - 1) % SEM_LIMIT + 1, (e, k)

        def run_engine(e, eng):
            waited = {}
            for o in by_eng[e]:
                w = {}
                hs = {}
                for d in o.deps:
                    h, v, key = sem_of(d.eng, d.sig)
                    if v > w.get(key, 0):
                        w[key] = v
                        hs[key] = h
                for ds, v in o.dma_targets.items():
                    key = ("d", id(ds))
                    if v > w.get(key, 0):
                        w[key] = v
                        hs[key] = ds.h
                for key, v in w.items():
                    if waited.get(key, 0) >= v:
                        continue
                    eng.wait_ge(hs[key], v)
                    waited[key] = v
                ins = o.fn(eng)
                if o.dsem is not None:
                    ins.then_inc(o.dsem.h, 16)
                elif o.needs_signal:
                    h, v, key = sem_of(o.eng, o.sig)
                    ins.then_inc(h, 1)

        with nc.Block() as block:
            for e in ENGS:
                if by_eng[e]:
                    getattr(block, e)(lambda eng, e=e: run_engine(e, eng))
        return {e: len(by_eng[e]) for e in ENGS}


def I(method, *a, **kw):
    return lambda e: getattr(e, method)(*a, **kw)


def mm(out_ap, pairs, start=True, stop=True):
    def fn(e):
        n = len(pairs)
        ins = None
        for i, (l, r) in enumerate(pairs):
            ins = e.matmul(out_ap, lhsT=l, rhs=r, start=(start and i == 0), stop=(stop and i == n - 1),
                           skip_group_check=True)
        return ins
    return fn


def build(nl=4, dbg=None):
    nc = bass.Bass("TRN2", target_bir_lowering=False)

    def din(name, shape, dt=F32):
        return nc.dram_tensor(name, list(shape), dt, kind="ExternalInput").ap()

    xT = din("xT", [D, SEQ])
    metaT = din("metaT", [D, NMETA])
    a_w_in = din("a_w_in", [2, D, 2 * DRNN])
    a_w_r = din("a_w_r", [2, NBLK, BLK, BLK])
    a_w_i = din("a_w_i", [2, NBLK, BLK, BLK])
    a_w_out = din("a_w_out", [2, DRNN, D])
    kv_w = din("kv_w", [D, 2 * D + NH])
    b_w_in = din("b_w_in", [2, D, 2 * D])
    b_w_out = din("b_w_out", [2, D, D])
    f_w_in = din("f_w_in", [4, D, 2 * DFF])
    f_w_out = din("f_w_out", [4, DFF, D])
    pa_d = din("pa", [BLK, 2 * NBLK * 8])
    pf_d = din("pf", [128, 4 * 44 * 4])
    pl_d = din("pl", [128, 4 * 8 * 4])
    fb_d = din("fb", [NH, 1])
    outT = nc.dram_tensor("outT", [D, SEQ], F32, kind="ExternalOutput").ap()
    kaug = nc.dram_tensor("kaug", [NH, KAUG, L], BF16).ap()
    qc = nc.dram_tensor("qc", [NH, 6, L], BF16).ap()
    vaug = nc.dram_tensor("vaug", [NH, 128, NTT, DH + 1], BF16).ap()

    st = ExitStack()
    P = Prog(nc, st)

    h = P.sb([128, NK, L], F32)
    hb = P.sb([128, NK, PAD + L], BF16)
    wall = P.sb([128, 2 * WH], BF16)
    wsm = [P.sb([128, 1024], BF16) for _ in range(2)]
    slots = [P.sb([128, SW], F32) for _ in range(NSLOT)]
    obuf = [P.sb([128, 4 * SW], BF16) for _ in range(2)]
    kbuf = [P.sb([128, L], BF16) for _ in range(2)]
    qbuf = [P.sb([128, L], BF16) for _ in range(2)]
    vbuf = [P.sb([128, NTT * (DH + 1)], BF16) for _ in range(2)]
    pa_t = P.sb([BLK, 2 * NBLK * 8], F32)
    pf_t = P.sb([128, 4 * 44 * 4], F32)
    pl_t = P.sb([128, 4 * 8 * 4], F32)
    pd_t = P.sb([BLK, 6 * 2 * NBLK], F32)
    carry = P.sb([BLK, NBLK], F32)
    fb_t = P.sb([128, 4], F32)
    cst = P.sb([128, SW], F32)
    cst2 = P.sb([128, SW], F32)
    onesf = P.sb([128, 64], F32)
    onesm = P.sb([128, 128], BF16)
    ident = P.sb([128, 128], BF16)
    maskT = P.sb([128, 128], BF16)
    zer = P.sb([128, 128], F32)
    one1 = P.sb([128, 128], F32)
    banks = [P.ps([128, 512]) for _ in range(8)]

    pa_v = pa_t[:, :].rearrange("c (l n e) -> c l n e", l=2, n=NBLK)
    pf_v = pf_t[:, :].rearrange("p (l j e) -> p l j e", l=4, j=44)
    pl_v = pl_t[:, :].rearrange("p (l k e) -> p l k e", l=4, k=NK)
    pd_v = pd_t[:, :].rearrange("c (q l n) -> c q l n", q=6, l=2)

    st_ = {"bank": 0, "slot": 0, "job": 0, "pb": 0, "sbk": 0, "ob": 0, "pjb": 0, "pt": 0}

    def bank(pool=None):
        if pool is None:
            i = st_["bank"] % 8
            st_["bank"] += 1
        elif pool == "S":
            i = st_["sbk"] % 4
            st_["sbk"] += 1
        elif pool == "J":
            i = 4 + st_["pjb"] % 2
            st_["pjb"] += 1
        else:
            i = 6 + st_["ob"] % 2
            st_["ob"] += 1
        return banks[i], ("bank", i)

    def slot():
        i = st_["slot"] % NSLOT
        st_["slot"] += 1
        return slots[i], ("slot", i)

    def chunks_of(t0, t1):
        t0 = max(t0, 0)
        return range(t0 // N, min((t1 - 1) // N, NCH - 1) + 1)

    def hbkeys(t0, t1):
        return [("hb", k, c) for k in range(NK) for c in chunks_of(t0, t1)]

    def hkeys(t0, t1, k=None):
        ks = range(NK) if k is None else [k]
        return [("h", kk, c) for kk in ks for c in chunks_of(t0, t1)]

    def wkeys(half):
        return [("w", u) for u in range(6 * half, 6 * half + 6)]

    ld = P.dsem("ld")
    wsem = [P.dsem("w0"), P.dsem("w1")]
    wssem = [P.dsem("ws0"), P.dsem("ws1")]
    ksem = [P.dsem("k0"), P.dsem("k1")]
    stsem = P.dsem("st")
    outsem = P.dsem("out")

    def dma(eng, out_ap, in_ap, reads, writes, dsem):
        return P.op(eng, I("dma_start", out=out_ap, in_=in_ap), reads, writes, dsem)

    P.op("vector", I("memset", cst[:, :], -0.5), writes=["cst"])
    P.op("vector", I("memset", cst2[:, :], 0.5), writes=["cst2"])
    P.op("vector", I("memset", onesf[:, :], 1.0), writes=["onesf"])
    P.op("vector", I("memset", onesm[:, :], 1.0 / 1024.0), writes=["onesm"])
    P.op("vector", I("memset", zer[:, :], 0.0), writes=["zer"])
    P.op("vector", I("memset", one1[:, :], 1.0), writes=["one1"])
    P.op("vector", I("memset", carry[:, :], 0.0), writes=["carry"])
    for k in range(NK):
        P.op("gpsimd", I("memset", hb[:, k, 0:PAD], 0.0), writes=[("hbpad", k)])
    P.op("gpsimd", I("affine_select", out=ident[:, :], in_=one1[:, :], pattern=[[1, 128]],
                                             compare_op=ALU.is_equal, fill=0.0, base=0, channel_multiplier=-1),
         reads=["one1"], writes=["ident"])
    P.op("gpsimd", I("affine_select", out=maskT[:, :], in_=zer[:, :], pattern=[[1, 128]],
                                             compare_op=ALU.is_ge, fill=-30000.0, base=0, channel_multiplier=-1),
         reads=["zer"], writes=["maskT"])
    dma("sync", pa_t[:, :], pa_d, [], ["pa"], ld)
    dma("sync", pf_t[:, :], pf_d, [], ["pf"], ld)
    dma("sync", pl_t[:, :], pl_d, [], ["pl"], ld)
    dma("sync", fb_t[0:NH, 0:1], fb_d, [], ["fb"], ld)
    P.op("vector", I("memset", fb_t[:, 3:4], 1e-30), writes=["tiny"])
    dma("sync", h[:, :, 0:NMETA], metaT.rearrange("(k p) t -> p k t", p=128), [], ["hmeta"], ld)
    for k in range(NK):
        dma("sync", h[:, k, NMETA:L], xT[k * 128:(k + 1) * 128, :], ["hmeta"], hkeys(0, L, k), ld)
    for k in range(NK):
        for c in range(NCH):
            eng = ["vector", "gpsimd", "scalar"][(k * NCH + c) % 3]
            t0 = c * N
            if eng == "scalar":
                P.op(eng, I("activation", out=hb[:, k, PAD + t0:PAD + t0 + N], in_=h[:, k, t0:t0 + N],
                                                             func=AF.Copy),
                     reads=[("h", k, c)], writes=[("hb", k, c)])
            else:
                P.op(eng, I("tensor_copy", out=hb[:, k, PAD + t0:PAD + t0 + N], in_=h[:, k, t0:t0 + N]),
                     reads=[("h", k, c)], writes=[("hb", k, c)])
    lam_v = pa_v[:, :, :, 7]
    P.op("scalar", I("activation", out=pd_v[:, 4], in_=lam_v, func=AF.Exp, scale=-1.0), reads=["pa"], writes=["pd4"])
    P.op("scalar", I("activation", out=pd_v[:, 5], in_=pd_v[:, 4], func=AF.Ln, bias=1.0), reads=["pd4"], writes=["pd5"])
    P.op("vector", I("tensor_scalar", out=pd_v[:, 0], in0=pd_v[:, 5], scalar1=-4.0, scalar2=None, op0=ALU.mult),
         reads=["pd5"], writes=["pd"])
    P.op("vector", I("tensor_scalar", out=pd_v[:, 1], in0=pd_v[:, 5], scalar1=4.0, scalar2=None, op0=ALU.mult),
         reads=["pd5"], writes=["pd"])
    P.op("vector", I("tensor_scalar", out=pd_v[:, 2], in0=pa_v[:, :, :, 5], scalar1=0.5, scalar2=None, op0=ALU.mult),
         reads=["pa"], writes=["pd"])
    P.op("vector", I("tensor_scalar", out=pd_v[:, 3], in0=pa_v[:, :, :, 6], scalar1=0.5, scalar2=None, op0=ALU.mult),
         reads=["pa"], writes=["pd"])
    P.op("vector", I("tensor_scalar", out=fb_t[0:NH, 1:2], in0=fb_t[0:NH, 0:1], scalar1=-1.0, scalar2=None, op0=ALU.mult),
         reads=["fb"], writes=["nfb"])

    def layer_norm(l, which):
        gi, bi = 2 * which, 2 * which + 1
        for c in range(NCH):
            t0 = c * N
            b1, b1k = bank()
            b2, b2k = bank()
            for half in range(2):
                for kk in range(4):
                    k = half * 4 + kk
                    P.op("gpsimd", I("tensor_copy", out=obuf[0][:, kk * SW:kk * SW + N], in_=h[:, k, t0:t0 + N]),
                         reads=[("h", k, c)], writes=[("ob", 0, kk)])
                    P.op("scalar", I("activation", out=obuf[1][:, kk * SW:kk * SW + N], in_=h[:, k, t0:t0 + N],
                                                                      func=AF.Square),
                         reads=[("h", k, c)], writes=[("ob", 1, kk)])
                P.op("tensor", mm(b1[:, 0:N], [(onesm[:, :], obuf[0][:, kk * SW:kk * SW + N]) for kk in range(4)],
                                  start=(half == 0), stop=(half == 1)),
                     reads=["onesm"] + [("ob", 0, kk) for kk in range(4)], writes=[b1k])
                P.op("tensor", mm(b2[:, 0:N], [(onesm[:, :], obuf[1][:, kk * SW:kk * SW + N]) for kk in range(4)],
                                  start=(half == 0), stop=(half == 1)),
                     reads=["onesm"] + [("ob", 1, kk) for kk in range(4)], writes=[b2k])
            mean, mk = slot()
            msq, qk = slot()
            vpe, vk = slot()
            P.op("vector", I("tensor_copy", out=mean[:, 0:N], in_=b1[:, 0:N]), reads=[b1k], writes=[mk])
            P.op("gpsimd", I("tensor_tensor", out=msq[:, 0:N], in0=mean[:, 0:N], in1=mean[:, 0:N], op=ALU.mult),
                 reads=[mk], writes=[qk])
            P.op("vector", I("scalar_tensor_tensor", out=vpe[:, 0:N], in0=b2[:, 0:N], scalar=EPS, in1=msq[:, 0:N],
                                                            op0=ALU.add, op1=ALU.subtract),
                 reads=[b2k, qk], writes=[vk])
            P.op("scalar", I("activation", out=vpe[:, 0:N], in_=vpe[:, 0:N], func=AF.Ln), reads=[vk], writes=[vk])
            P.op("scalar", I("activation", out=vpe[:, 0:N], in_=vpe[:, 0:N], func=AF.Exp, scale=-0.5), reads=[vk], writes=[vk])
            for k in range(NK):
                t1, tk = slot()
                P.op("vector", I("tensor_tensor", out=t1[:, 0:N], in0=h[:, k, t0:t0 + N], in1=mean[:, 0:N],
                                                                     op=ALU.subtract),
                     reads=[("h", k, c), mk], writes=[tk])
                P.op("vector", I("tensor_tensor", out=t1[:, 0:N], in0=t1[:, 0:N], in1=vpe[:, 0:N], op=ALU.mult),
                     reads=[tk, vk], writes=[tk])
                P.op("scalar", I("activation", out=h[:, k, t0:t0 + N], in_=t1[:, 0:N], func=AF.Identity,
                                                                  scale=pl_v[:, l, k, gi:gi + 1], bias=pl_v[:, l, k, bi:bi + 1]),
                     reads=[tk, "pl"], writes=[("h", k, c)])
                P.op("gpsimd", I("tensor_scalar", out=hb[:, k, PAD + t0:PAD + t0 + N], in0=t1[:, 0:N],
                                                                     scalar1=pl_v[:, l, k, gi:gi + 1], scalar2=pl_v[:, l, k, bi:bi + 1],
                                                                     op0=ALU.mult, op1=ALU.add),
                     reads=[tk, "pl"], writes=[("hb", k, c)])

    def accum(m, c, pb, pbk, first):
        t0 = c * N
        if first:
            P.op("vector", I("scalar_tensor_tensor", out=h[:, m, t0:t0 + N], in0=h[:, m, t0:t0 + N], scalar=ALPHA,
                                                            in1=pb[:, 0:N], op0=ALU.mult, op1=ALU.add),
                 reads=[("h", m, c), pbk], writes=[("h", m, c)])
        else:
            P.op("vector", I("tensor_tensor", out=h[:, m, t0:t0 + N], in0=h[:, m, t0:t0 + N], in1=pb[:, 0:N], op=ALU.add),
                 reads=[("h", m, c), pbk], writes=[("h", m, c)])

    jobs = []

    def a_load(l, s, half):
        base = half * WH
        wk_ = wkeys(half)
        wg = wall[:, base:base + 3072].rearrange("p (k c) -> p k c", k=NK)
        wr = wall[:, base + 3072:base + 6144].rearrange("p (k c) -> p k c", k=NK)
        gr = wall[0:BLK, base + 6144:base + 6528].rearrange("c (n d) -> c n d", n=4)
        gi_ = wall[0:BLK, base + 6528:base + 6912].rearrange("c (n d) -> c n d", n=4)
        wo = wall[0:BLK, base + 6912:base + 6912 + 4096].rearrange("c (n m) -> c n m", n=4)
        ds = wsem[half]
        dma("gpsimd", wg, a_w_in[l, :, 384 * s:384 * s + 384].rearrange("(k p) c -> p k c", p=128), [], wk_, ds)
        dma("gpsimd", wr, a_w_in[l, :, DRNN + 384 * s:DRNN + 384 * s + 384].rearrange("(k p) c -> p k c", p=128), [], wk_, ds)
        dma("gpsimd", gr, a_w_r[l, 4 * s:4 * s + 4].rearrange("n c d -> c n d"), [], wk_, ds)
        dma("gpsimd", gi_, a_w_i[l, 4 * s:4 * s + 4].rearrange("n c d -> c n d"), [], wk_, ds)
        dma("gpsimd", wo, a_w_out[l, 384 * s:384 * s + 384, :].rearrange("(n c) m -> c n m", c=BLK), [], wk_, ds)

    def a_compute(l, s, half):
        base = half * WH
        wk_ = wkeys(half)
        wg = wall[:, base:base + 3072].rearrange("p (k c) -> p k c", k=NK)
        wr = wall[:, base + 3072:base + 6144].rearrange("p (k c) -> p k c", k=NK)
        gr = wall[0:BLK, base + 6144:base + 6528].rearrange("c (n d) -> c n d", n=4)
        gi_ = wall[0:BLK, base + 6528:base + 6912].rearrange("c (n d) -> c n d", n=4)
        wo = wall[0:BLK, base + 6912:base + 6912 + 4096].rearrange("c (n m) -> c n m", n=4)
        for c in range(NCH):
            t0 = c * N
            obi = st_["pb"] % 2
            st_["pb"] += 1
            og = obuf[obi]
            for nb in range(4):
                n = 4 * s + nb
                cs = slice(nb * BLK, (nb + 1) * BLK)
                pg, pgk = bank()
                pr, prk = bank()
                P.op("tensor", mm(pg[0:BLK, 0:N], [(wg[:, k, cs], hb[:, k, PAD + t0:PAD + t0 + N]) for k in range(NK)]),
                     reads=wk_ + hbkeys(t0, t0 + N), writes=[pgk])
                P.op("tensor", mm(pr[0:BLK, 0:N + 3], [(wr[:, k, cs], hb[:, k, t0:t0 + N + 3]) for k in range(NK)]),
                     reads=wk_ + hbkeys(t0 - 3, t0 + N) + [("hbpad", k) for k in range(NK)], writes=[prk])
                gg, ggk = slot()
                xc, xck = slot()
                tr, trk = slot()
                ti, tik = slot()
                aa, aak = slot()
                tl, tlk = slot()
                hs_, hsk = slot()
                xcb = kbuf[nb % 2]
                xcbk = ("kbuf", nb % 2)
                cw = lambda j: pa_v[:, l, n, j:j + 1]
                P.op("scalar", I("activation", out=xc[0:BLK, 0:N], in_=pr[0:BLK, 3:N + 3], func=AF.Identity,
                                                      scale=cw(3), bias=cw(4)), reads=[prk, "pa"], writes=[xck])
                P.op("scalar", I("activation", out=gg[0:BLK, 0:N], in_=pg[0:BLK, 0:N], func=AF.Gelu_apprx_tanh),
                     reads=[pgk], writes=[ggk])
                for j in (2, 1, 0):
                    P.op("vector", I("scalar_tensor_tensor", out=xc[0:BLK, 0:N], in0=pr[0:BLK, j:j + N], scalar=cw(j),
                                                                         in1=xc[0:BLK, 0:N], op0=ALU.mult, op1=ALU.add),
                         reads=[prk, xck, "pa"], writes=[xck])
                P.op("gpsimd", I("tensor_copy", out=xcb[0:BLK, 0:N], in_=xc[0:BLK, 0:N]), reads=[xck], writes=[xcbk])
                p1, p1k = bank()
                p2, p2k = bank()
                P.op("tensor", mm(p1[0:BLK, 0:N], [(gr[:, nb, :], xcb[0:BLK, 0:N])]), reads=wk_ + [xcbk], writes=[p1k])
                P.op("tensor", mm(p2[0:BLK, 0:N], [(gi_[:, nb, :], xcb[0:BLK, 0:N])]), reads=wk_ + [xcbk], writes=[p2k])
                P.op("scalar", I("activation", out=tr[0:BLK, 0:N], in_=p1[0:BLK, 0:N], func=AF.Tanh, scale=0.5,
                                                      bias=pd_v[:, 2, l, n:n + 1]), reads=[p1k, "pd"], writes=[trk])
                P.op("scalar", I("activation", out=ti[0:BLK, 0:N], in_=p2[0:BLK, 0:N], func=AF.Tanh, scale=0.5,
                                                      bias=pd_v[:, 3, l, n:n + 1]), reads=[p2k, "pd"], writes=[tik])
                P.op("scalar", I("activation", out=tl[0:BLK, 0:N], in_=tr[0:BLK, 0:N], func=AF.Tanh,
                                                      scale=pd_v[:, 1, l, n:n + 1], bias=pd_v[:, 1, l, n:n + 1]),
                     reads=[trk, "pd"], writes=[tlk])
                P.op("scalar", I("activation", out=aa[0:BLK, 0:N], in_=tr[0:BLK, 0:N], func=AF.Exp,
                                                      scale=pd_v[:, 0, l, n:n + 1], bias=pd_v[:, 0, l, n:n + 1]),
                     reads=[trk, "pd"], writes=[aak])
                P.op("scalar", I("activation", out=tr[0:BLK, 0:N], in_=aa[0:BLK, 0:N], func=AF.Square),
                     reads=[aak, trk], writes=[trk])
                P.op("vector", I("scalar_tensor_tensor", out=ti[0:BLK, 0:N], in0=ti[0:BLK, 0:N], scalar=1.0,
                                                                in1=xc[0:BLK, 0:N], op0=ALU.add, op1=ALU.mult),
                     reads=[tik, xck], writes=[tik])
                P.op("vector", I("scalar_tensor_tensor", out=tl[0:BLK, 0:N], in0=tr[0:BLK, 0:N], scalar=1.0,
                                                                in1=tl[0:BLK, 0:N], op0=ALU.add, op1=ALU.mult),
                     reads=[trk, tlk], writes=[tlk])
                P.op("scalar", I("activation", out=tl[0:BLK, 0:N], in_=tl[0:BLK, 0:N], func=AF.Ln, bias=fb_t[0:BLK, 3:4]),
                     reads=[tlk, "tiny"], writes=[tlk])
                P.op("scalar", I("activation", out=tl[0:BLK, 0:N], in_=tl[0:BLK, 0:N], func=AF.Exp, scale=0.5), reads=[tlk], writes=[tlk])
                P.op("gpsimd", I("tensor_tensor", out=ti[0:BLK, 0:N], in0=ti[0:BLK, 0:N], in1=tl[0:BLK, 0:N], op=ALU.mult),
                     reads=[tik, tlk], writes=[tik])
                ck = ("carry", n)
                P.op("vector", I("tensor_tensor_scan", out=hs_[0:BLK, 0:N], data0=aa[0:BLK, 0:N], data1=ti[0:BLK, 0:N],
                                                              initial=carry[:, n:n + 1], op0=ALU.mult, op1=ALU.add),
                     reads=[aak, tik, ck, "carry"], writes=[hsk])
                P.op("gpsimd", I("tensor_copy", out=carry[:, n:n + 1], in_=hs_[0:BLK, N - 1:N]), reads=[hsk], writes=[ck])
                P.op("vector", I("scalar_tensor_tensor", out=og[0:BLK, nb * SW:nb * SW + N], in0=hs_[0:BLK, 0:N], scalar=0.5,
                                                                in1=gg[0:BLK, 0:N], op0=ALU.mult, op1=ALU.mult),
                     reads=[hsk, ggk], writes=[("ob", obi, nb)])
            for m in range(NK):
                po, pok = bank()
                P.op("tensor", mm(po[:, 0:N], [(wo[:, nb, m * 128:(m + 1) * 128], og[0:BLK, nb * SW:nb * SW + N]) for nb in range(4)]),
                     reads=wk_ + [("ob", obi, nb) for nb in range(4)], writes=[pok])
                accum(m, c, po, pok, first=(s == 0))
        if s == 3:
            P.op("vector", I("memset", carry[:, :], 0.0), reads=[("carry", n) for n in range(NBLK)],
                 writes=["carry"] + [("carry", n) for n in range(NBLK)])
            layer_norm(l, 0)

    def f_load(l, si, half):
        J = FFN_SLICES[si]
        kk = len(J)
        base = half * WH
        wk_ = wkeys(half)
        ds = wsem[half]
        wfg = wall[:, base:base + NK * kk * 128].rearrange("p (k c) -> p k c", k=NK)
        wfv = wall[:, base + 4096:base + 4096 + NK * kk * 128].rearrange("p (k c) -> p k c", k=NK)
        wfo = wall[:, base + 8192:base + 8192 + kk * 1024].rearrange("p (j m) -> p j m", j=kk)
        j0 = J[0]
        dma("gpsimd", wfg, f_w_in[l, :, 128 * j0:128 * (j0 + kk)].rearrange("(k p) c -> p k c", p=128), [], wk_, ds)
        dma("gpsimd", wfv, f_w_in[l, :, DFF + 128 * j0:DFF + 128 * (j0 + kk)].rearrange("(k p) c -> p k c", p=128), [], wk_, ds)
        dma("gpsimd", wfo, f_w_out[l, 128 * j0:128 * (j0 + kk), :].rearrange("(j p) m -> p j m", p=128), [], wk_, ds)

    def f_compute(l, si, half):
        J = FFN_SLICES[si]
        kk = len(J)
        base = half * WH
        wk_ = wkeys(half)
        wfg = wall[:, base:base + NK * kk * 128].rearrange("p (k c) -> p k c", k=NK)
        wfv = wall[:, base + 4096:base + 4096 + NK * kk * 128].rearrange("p (k c) -> p k c", k=NK)
        wfo = wall[:, base + 8192:base + 8192 + kk * 1024].rearrange("p (j m) -> p j m", j=kk)
        for c in range(NCH):
            t0 = c * N
            obi = st_["pb"] % 2
            st_["pb"] += 1
            pr_ = obuf[obi]
            for jj, j in enumerate(J):
                cs = slice(jj * 128, (jj + 1) * 128)
                pg, pgk = bank()
                pv, pvk = bank()
                rk = wk_ + hbkeys(t0 - 2, t0 + N) + [("hbpad", k) for k in range(NK)]
                P.op("tensor", mm(pg[:, 0:N + 2], [(wfg[:, k, cs], hb[:, k, t0 + 1:t0 + N + 3]) for k in range(NK)]), reads=rk, writes=[pgk])
                P.op("tensor", mm(pv[:, 0:N + 2], [(wfv[:, k, cs], hb[:, k, t0 + 1:t0 + N + 3]) for k in range(NK)]), reads=rk, writes=[pvk])
                gc, gck = slot()
                vc, vck = slot()
                fw = lambda jx, e_: pf_v[:, l, jx, e_:e_ + 1]
                jv = NPAIR + j
                P.op("scalar", I("activation", out=gc[:, 0:N], in_=pg[:, 2:N + 2], func=AF.Identity, scale=fw(j, 2), bias=fw(j, 3)),
                     reads=[pgk, "pf"], writes=[gck])
                P.op("scalar", I("activation", out=vc[:, 0:N], in_=pv[:, 2:N + 2], func=AF.Identity, scale=fw(jv, 2), bias=fw(jv, 3)),
                     reads=[pvk, "pf"], writes=[vck])
                for tap in (1, 0):
                    P.op("vector", I("scalar_tensor_tensor", out=gc[:, 0:N], in0=pg[:, tap:tap + N], scalar=fw(j, tap),
                                                                             in1=gc[:, 0:N], op0=ALU.mult, op1=ALU.add),
                         reads=[pgk, gck, "pf"], writes=[gck])
                    P.op("vector", I("scalar_tensor_tensor", out=vc[:, 0:N], in0=pv[:, tap:tap + N], scalar=fw(jv, tap),
                                                                             in1=vc[:, 0:N], op0=ALU.mult, op1=ALU.add),
                         reads=[pvk, vck, "pf"], writes=[vck])
                P.op("scalar", I("activation", out=gc[:, 0:N], in_=gc[:, 0:N], func=AF.Gelu_apprx_tanh), reads=[gck], writes=[gck])
                P.op("gpsimd", I("tensor_tensor", out=pr_[:, jj * SW:jj * SW + N], in0=gc[:, 0:N], in1=vc[:, 0:N], op=ALU.mult),
                     reads=[gck, vck], writes=[("ob", obi, jj)])
            for m in range(NK):
                po, pok = bank()
                P.op("tensor", mm(po[:, 0:N], [(wfo[:, jj, m * 128:(m + 1) * 128], pr_[:, jj * SW:jj * SW + N]) for jj in range(kk)]),
                     reads=wk_ + [("ob", obi, jj) for jj in range(kk)], writes=[pok])
                accum(m, c, po, pok, first=(si == 0))
        if si == len(FFN_SLICES) - 1:
            layer_norm(l, 1)

    def k_load(half):
        base = 0
        wk_ = wkeys(0)
        wk = wall[:, base:base + 8192].rearrange("p (k c) -> p k c", k=NK)
        wf = wall[:, base + 8192:base + 8192 + 128].rearrange("p (k c) -> p k c", k=NK)
        dma("gpsimd", wk, kv_w[:, 0:D].rearrange("(k p) c -> p k c", p=128), [], wk_, wsem[0])
        dma("gpsimd", wf, kv_w[:, 2 * D:2 * D + NH].rearrange("(k p) c -> p k c", p=128), [], wk_, wsem[0])

    def k_compute(half):
        wk_ = wkeys(0)
        wk = wall[:, 0:8192].rearrange("p (k c) -> p k c", k=NK)
        wf = wall[:, 8192:8192 + 128].rearrange("p (k c) -> p k c", k=NK)
        onesrow = wall[0:NH, 11 * UNIT:12 * UNIT]
        P.op("vector", I("memset", onesrow, 1.0), writes=[("w", 11)])
        for j in range(3):
            dma("sync", kaug[:, 64 + j, :], onesrow, [("w", 11)], [("kaug_c", j)], stsem)
        P.op("vector", I("memset", onesrow, -1.0), writes=[("w", 11)])
        for j in range(3):
            dma("sync", qc[:, 3 + j, :], onesrow, [("w", 11)], [("qc_c", 3 + j)], stsem)
        P.op("vector", I("memset", fb_t[0:NH, 2:3], 0.0), writes=["ccarry"])
        for c in range(NCH):
            t0 = c * N
            pz, pzk = bank()
            P.op("tensor", mm(pz[0:NH, 0:N], [(wf[:, k, :], hb[:, k, PAD + t0:PAD + t0 + N]) for k in range(NK)]),
                 reads=wk_ + hbkeys(t0, t0 + N), writes=[pzk])
            e1, e1k = slot()
            cc, cck = slot()
            r1, r1k = slot()
            P.op("scalar", I("activation", out=e1[0:NH, 0:N], in_=pz[0:NH, 0:N], func=AF.Exp, scale=-1.0, bias=fb_t[0:NH, 1:2]),
                 reads=[pzk, "nfb"], writes=[e1k])
            P.op("scalar", I("activation", out=e1[0:NH, 0:N], in_=e1[0:NH, 0:N], func=AF.Ln, bias=1.0), reads=[e1k], writes=[e1k])
            on, onk = slot()
            P.op("gpsimd", I("memset", on[0:NH, 0:N], 1.0), writes=[onk])
            P.op("vector", I("tensor_tensor_scan", out=cc[0:NH, 0:N], data0=on[0:NH, 0:N], data1=e1[0:NH, 0:N],
                                                          initial=fb_t[0:NH, 2:3], op0=ALU.mult, op1=ALU.subtract),
                 reads=[e1k, onk, "ccarry"], writes=[cck])
            P.op("gpsimd", I("tensor_copy", out=fb_t[0:NH, 2:3], in_=cc[0:NH, N - 1:N]), reads=[cck], writes=["ccarry"])
            ob = obuf[c % 2]
            hi = ob[0:NH, 0:N]
            lo = ob[0:NH, SW:SW + N]
            ll = ob[0:NH, 2 * SW:2 * SW + N]
            okk = [("ob", c % 2, j) for j in range(3)]
            P.op("vector", I("tensor_copy", out=hi, in_=cc[0:NH, 0:N]), reads=[cck], writes=[okk[0]])
            P.op("vector", I("tensor_tensor", out=r1[0:NH, 0:N], in0=cc[0:NH, 0:N], in1=hi, op=ALU.subtract),
                 reads=[cck, okk[0]], writes=[r1k])
            P.op("vector", I("tensor_copy", out=lo, in_=r1[0:NH, 0:N]), reads=[r1k], writes=[okk[1]])
            P.op("vector", I("tensor_tensor", out=r1[0:NH, 0:N], in0=r1[0:NH, 0:N], in1=lo, op=ALU.subtract),
                 reads=[r1k, okk[1]], writes=[r1k])
            P.op("vector", I("tensor_copy", out=ll, in_=r1[0:NH, 0:N]), reads=[r1k], writes=[okk[2]])
            for j, src in enumerate((hi, lo, ll)):
                dma("sync", kaug[:, 67 + j, t0:t0 + N], src, [okk[j]], [("kaug_c", 3 + j, c)], stsem)
                dma("sync", qc[:, j, t0:t0 + N], src, [okk[j]], [("qc_c", j, c)], stsem)
        for hd in range(NH):
            kb = hd % 2
            for sb in range(5):
                q0 = 512 * sb
                q1 = min(q0 + 512, L)
                nq = q1 - q0
                pk, pkk = bank()
                P.op("tensor", mm(pk[0:DH, 0:nq], [(wk[:, k, hd * DH:(hd + 1) * DH], hb[:, k, PAD + q0:PAD + q1]) for k in range(NK)]),
                     reads=wk_ + hbkeys(q0, q1), writes=[pkk])
                if (hd * 5 + sb) % 2 == 0:
                    P.op("scalar", I("activation", out=qbuf[kb][0:DH, q0:q1], in_=pk[0:DH, 0:nq],
                                                                                            func=AF.Copy),
                         reads=[pkk], writes=[("qbuf", kb, sb)])
                else:
                    P.op("vector", I("tensor_copy", out=qbuf[kb][0:DH, q0:q1], in_=pk[0:DH, 0:nq]),
                         reads=[pkk], writes=[("qbuf", kb, sb)])
            dma("sync", kaug[hd, 0:DH, :], qbuf[kb][0:DH, :], [("qbuf", kb, sb) for sb in range(5)], [("kaug", hd)], stsem)

    def v_load(half):
        wk_ = [("w", u) for u in range(6, 10)]
        wv = wall[:, WH:WH + 8192].rearrange("p (k c) -> p k c", k=NK)
        dma("gpsimd", wv, kv_w[:, D:2 * D].rearrange("(k p) c -> p k c", p=128), [], wk_, wsem[1])

    def v_compute(half):
        wk_ = [("w", u) for u in range(6, 10)]
        wv = wall[:, WH:WH + 8192].rearrange("p (k c) -> p k c", k=NK)
        for i in range(2):
            v3 = obuf[i][:, 0:NH * (DH + 1)].rearrange("p (a b) -> p a b", a=NH)
            P.op("vector", I("memset", v3[:, :, DH:DH + 1], 1.0), writes=[("ob", i, j) for j in range(4)])
        for tt in range(NTT):
            tok0 = 128 * tt
            nt = min(128, L - tok0)
            i = tt % 2
            v3 = obuf[i][:, 0:NH * (DH + 1)].rearrange("p (a b) -> p a b", a=NH)
            okk = [("ob", i, j) for j in range(4)]
            for jv in range(2):
                pv, pvk = bank()
                P.op("tensor", mm(pv[0:nt, 0:512], [(hb[:, k, PAD + tok0:PAD + tok0 + nt], wv[:, k, 512 * jv:512 * jv + 512]) for k in range(NK)]),
                     reads=wk_ + hbkeys(tok0, tok0 + nt), writes=[pvk])
                src = pv[0:nt, 0:512].rearrange("p (a b) -> p a b", a=8)
                dst = v3[0:nt, 8 * jv:8 * jv + 8, 0:DH]
                if jv == 0:
                    P.op("scalar", I("activation", out=dst, in_=src, func=AF.Copy), reads=[pvk] + okk, writes=okk)
                else:
                    P.op("vector", I("tensor_copy", out=dst, in_=src), reads=[pvk] + okk, writes=okk)
            dma("sync", vaug[:, 0:nt, tt, :].rearrange("h p e -> p h e"), v3[0:nt, :, :], okk, [("vaug", tt)], stsem)

    def b_load(j, half):
        wbo = wall[:, 8 * UNIT:8 * UNIT + 8192].rearrange("p (k m) -> p k m", k=NK)
        dma("gpsimd", wbo, b_w_out[j].rearrange("(k p) m -> p k m", p=128), [], [("w", u) for u in range(8, 12)], wsem[1])

    def b_compute(l, j, half):
        wbo = wall[:, 8 * UNIT:8 * UNIT + 8192].rearrange("p (k m) -> p k m", k=NK)
        kvdeps = ([("kaug_c", jj) for jj in range(3)] + [("qc_c", 3 + jj) for jj in range(3)]
                  + [("kaug_c", 3 + jj, c) for jj in range(3) for c in range(NCH)]
                  + [("qc_c", jj, c) for jj in range(3) for c in range(NCH)]
                  + [("vaug", tt) for tt in range(NTT)])
        for hd in range(NH):
            kb = hd % 2
            w3 = wsm[kb][:, :].rearrange("p (k c) -> p k c", k=NK)
            dma("sync", kbuf[kb][0:KAUG, :], kaug[hd], kvdeps + [("kaug", hd)], [("kbuf", kb)], ksem[kb])
            dma("sync", vbuf[kb][:, :], vaug[hd].rearrange("p t e -> p (t e)"), kvdeps, [("vbuf", kb)], ksem[kb])
            dma("sync", qbuf[kb][DH:KAUG, :], qc[hd], kvdeps, [("qbufc", kb)], ksem[kb])
            dma("gpsimd", w3[:, :, 0:DH], b_w_in[j, :, hd * DH:(hd + 1) * DH].rearrange("(k p) c -> p k c", p=128), [], [("wsm", kb)], wssem[kb])
            dma("gpsimd", w3[:, :, DH:2 * DH], b_w_in[j, :, D + hd * DH:D + (hd + 1) * DH].rearrange("(k p) c -> p k c", p=128), [],
                [("wsm", kb)], wssem[kb])
            for sb in range(5):
                q0 = 512 * sb
                q1 = min(q0 + 512, L)
                nq = q1 - q0
                rk = [("wsm", kb)] + hbkeys(q0, q1)
                pq, pqk = bank("J")
                P.op("tensor", mm(pq[0:DH, 0:nq], [(w3[:, k, 0:DH], hb[:, k, PAD + q0:PAD + q1]) for k in range(NK)]), reads=rk, writes=[pqk])
                P.op("scalar", I("activation", out=qbuf[kb][0:DH, q0:q1], in_=pq[0:DH, 0:nq],
                                                                                func=AF.Identity, scale=0.125),
                     reads=[pqk], writes=[("qbuf", kb, sb)])
                pgt, pgtk = bank("J")
                P.op("tensor", mm(pgt[0:DH, 0:nq], [(w3[:, k, DH:2 * DH], hb[:, k, PAD + q0:PAD + q1]) for k in range(NK)]), reads=rk, writes=[pgtk])
                pO, pOk = bank("O")
                nkc = (q1 + 127) // 128
                for kc in range(nkc):
                    k0 = 128 * kc
                    ks = min(128, L - k0)
                    off = max(k0 - q0, 0)
                    diag = k0 >= q0
                    pS, pSk = bank("S")

                    def smm(e, pS=pS, k0=k0, ks=ks, off=off, diag=diag, kb=kb, q0=q0, q1=q1, nq=nq):
                        ins = e.matmul(pS[0:ks, off:nq], lhsT=kbuf[kb][0:KAUG, k0:k0 + ks], rhs=qbuf[kb][0:KAUG, q0 + off:q1],
                                       start=True, stop=not diag, skip_group_check=True)
                        if diag:
                            w_ = min(128, nq - off)
                            ins = e.matmul(pS[0:ks, off:off + w_], lhsT=ident[0:ks, 0:ks], rhs=maskT[0:ks, 0:w_],
                                           start=False, stop=True, skip_group_check=True)
                        return ins
                    P.op("tensor", smm, reads=[("kbuf", kb), ("qbuf", kb, sb), ("qbufc", kb), "ident", "maskT"], writes=[pSk])
                    pi = st_["pt"] % 4
                    st_["pt"] += 1
                    pT = obuf[pi // 2][:, (pi % 2) * 2 * SW:(pi % 2) * 2 * SW + 512]
                    pTk = [("ob", pi // 2, (pi % 2) * 2), ("ob", pi // 2, (pi % 2) * 2 + 1)]
                    P.op("scalar", I("activation", out=pT[0:ks, off:nq], in_=pS[0:ks, off:nq], func=AF.Exp),
                         reads=[pSk], writes=pTk)
                    P.op("tensor", mm(pO[0:DH + 1, off:nq], [(vbuf[kb][0:ks, kc * (DH + 1):(kc + 1) * (DH + 1)], pT[0:ks, off:nq])],
                                      start=(kc == 0), stop=(kc == nkc - 1)),
                         reads=[("vbuf", kb)] + pTk, writes=[pOk])
                for hh in range(2):
                    c0 = 256 * hh
                    c1 = min(c0 + 256, nq)
                    if c0 >= nq:
                        break
                    w_ = c1 - c0
                    rd, rdk = slot()
                    tg, tgk = slot()
                    tt_, ttk = slot()
                    P.op("vector", I("reciprocal", out=rd[DH:DH + 1, 0:w_], in_=pO[DH:DH + 1, c0:c1]),
                         reads=[pOk], writes=[rdk])
                    pB, pBk = bank("J")
                    P.op("tensor", mm(pB[0:DH, 0:w_], [(onesf[DH:DH + 1, 0:DH], rd[DH:DH + 1, 0:w_])]), reads=[rdk, "onesf"], writes=[pBk])
                    P.op("scalar", I("activation", out=tg[0:DH, 0:w_], in_=pgt[0:DH, c0:c1], func=AF.Tanh, scale=0.5),
                         reads=[pgtk], writes=[tgk])
                    P.op("vector", I("scalar_tensor_tensor", out=tt_[0:DH, 0:w_], in0=tg[0:DH, 0:w_], scalar=1.0, in1=pO[0:DH, c0:c1], op0=ALU.add, op1=ALU.mult),
                        reads=[tgk, pOk], writes=[ttk])
                    pp = (hd % 2) * DH
                    col = (hd // 2) * UNIT + q0 + c0
                    P.op("vector", I("scalar_tensor_tensor", out=wall[pp:pp + DH, col:col + w_], in0=tt_[0:DH, 0:w_], scalar=0.5, in1=pB[0:DH, 0:w_], op0=ALU.mult, op1=ALU.mult),
                        reads=[ttk, pBk], writes=[("w", hd // 2)])
        for c in range(NCH):
            t0 = c * N
            for m in range(NK):
                po, pok = bank()
                P.op("tensor", mm(po[:, 0:N], [(wbo[:, k, m * 128:(m + 1) * 128], wall[:, k * UNIT + t0:k * UNIT + t0 + N]) for k in range(NK)]),
                     reads=[("w", u) for u in range(12)], writes=[pok])
                accum(m, c, po, pok, first=True)
        layer_norm(l, 0)

    for l in range(nl):
        if l < 2:
            for s in range(4):
                jobs.append((lambda hf, l=l, s=s: a_load(l, s, hf), lambda hf, l=l, s=s: a_compute(l, s, hf), False))
        else:
            if l == 2:
                jobs.append((lambda hf: k_load(hf), lambda hf: k_compute(hf), True))
                jobs.append((lambda hf: v_load(hf), lambda hf: v_compute(hf), True))
            jobs.append((lambda hf, l=l: b_load(l - 2, hf), lambda hf, l=l: b_compute(l, l - 2, hf), True))
        if dbg == "mix" and l == nl - 1:
            break
        for si in range(len(FFN_SLICES)):
            jobs.append((lambda hf, l=l, si=si: f_load(l, si, hf), lambda hf, l=l, si=si: f_compute(l, si, hf), False))

    halves = [i % 2 for i in range(len(jobs))]
    jobs[0][0](halves[0])
    for i, (ldf, cf, ex) in enumerate(jobs):
        nxt = i + 1 < len(jobs)
        pre = nxt and not ex and not jobs[i + 1][2]
        if pre:
            jobs[i + 1][0](halves[i + 1])
        cf(halves[i])
        if nxt and not pre:
            jobs[i + 1][0](halves[i + 1])

    for k in range(NK):
        dma("sync", outT[k * 128:(k + 1) * 128, :], h[:, k, NMETA:L], hkeys(0, L, k), [("out", k)], outsem)
    P.op("sync", lambda e: None, reads=[("out", k) for k in range(NK)])
    counts = P.emit()
    st.close()
    return nc, counts


def host_layout(inp):
    f32 = np.float32
    xT = np.ascontiguousarray(np.transpose(inp["x"], (0, 2, 1)).astype(f32))
    metaT = np.ascontiguousarray(inp["meta"].T.astype(f32))
    cw = np.transpose(inp["a_conv_w"], (0, 2, 1))
    pa = np.concatenate([cw, inp["a_conv_b"][..., None], inp["a_b_r"][..., None], inp["a_b_i"][..., None],
                         inp["a_lambda"][..., None]], axis=-1)
    pa = np.ascontiguousarray(pa.reshape(2, NBLK, BLK, 8).transpose(2, 0, 1, 3)).reshape(BLK, -1).astype(f32)
    pf = np.concatenate([np.transpose(inp["f_conv_w"], (0, 2, 1)), inp["f_conv_b"][..., None]], axis=-1)
    pf = np.ascontiguousarray(pf.reshape(4, 44, 128, 4).transpose(2, 0, 1, 3)).reshape(128, -1).astype(f32)
    pl = np.stack([inp["ln1_g"], inp["ln1_b"], inp["ln2_g"], inp["ln2_b"]], axis=-1)
    pl = np.ascontiguousarray(pl.reshape(4, NK, 128, 4).transpose(2, 0, 1, 3)).reshape(128, -1).astype(f32)
    shared = {
        "metaT": metaT,
        "a_w_in": np.ascontiguousarray(inp["a_w_in"], dtype=f32),
        "a_w_r": np.ascontiguousarray(inp["a_w_r"], dtype=f32),
        "a_w_i": np.ascontiguousarray(inp["a_w_i"], dtype=f32),
        "a_w_out": np.ascontiguousarray(inp["a_w_out"], dtype=f32),
        "kv_w": np.ascontiguousarray(inp["kv_w"], dtype=f32),
        "b_w_in": np.ascontiguousarray(inp["b_w_in"], dtype=f32),
        "b_w_out": np.ascontiguousarray(inp["b_w_out"], dtype=f32),
        "f_w_in": np.ascontiguousarray(inp["f_w_in"], dtype=f32),
        "f_w_out": np.ascontiguousarray(inp["f_w_out"], dtype=f32),
        "pa": pa, "pf": pf, "pl": pl,
        "fb": np.ascontiguousarray(inp["kv_f_b"].reshape(NH, 1), dtype=f32),
    }
    return xT, shared


def kernel(**inp):
    inp = {k: np.asarray(v) for k, v in inp.items()}
    xT, shared = host_layout(inp)
    B = xT.shape[0]
    nc, _ = build(4)
    in_maps = [dict(shared, xT=xT[b]) for b in range(B)]
    res = run_bass_kernel_spmd(nc, in_maps, core_ids=list(range(B)))
    out = np.stack([np.asarray(res.results[b]["outT"]).T for b in range(B)], axis=0)
    return np.ascontiguousarray(out.astype(np.float32))
```
